# Optimizing a Trainium2 kernel written in Bass

```python
import math
import jax, jax.numpy as jnp
from jax import lax
import numpy as np

D_MODEL = 1024
BATCH = 8
SEQ = 2048
DEPTH = 2

GRID_W = 64
CTX_LEN = 256
N_MIXERS = 2
N_HYENA = (DEPTH + 1) // 2
N_RGLRU = DEPTH // 2
D_FF = 2816
N_MOD = 9
MACARON = 0.5
NORM_EPS = 1e-6
POS_BASE = 10000.0
HY_WIDTH = D_MODEL
HY_SHORT = 3
HY_EMB = 33
HY_BANDS = (HY_EMB - 1) // 2
HY_FILTER_HIDDEN = 64
HY_FAST_DECAY = 0.3
HY_SLOW_DECAY = 1.5
HY_DECAY_TARGET = 1e-2
RG_WIDTH = D_MODEL
RG_HEADS = 4
RG_BLOCK = RG_WIDTH // RG_HEADS
RG_CONV = 4
RG_C = 8.0

kernel_name = "hybrid_hyena_rglru_prefix_dit"

F32 = jnp.float32


def rmsnorm(x, g):
    xf = x.astype(F32)
    y = xf * lax.rsqrt(jnp.mean(xf * xf, axis=-1, keepdims=True) + NORM_EPS)
    return (y * g.astype(F32)).astype(x.dtype)


def modulate(x, shift, scale):
    return x * (1 + scale) + shift


def dwconv(u, w, b):
    K = w.shape[0]
    L = u.shape[1]
    lo = (K - 1) // 2
    hi = K - 1 - lo
    up = jnp.pad(u, ((0, 0), (lo, hi), (0, 0)))
    out = b
    for k in range(K):
        out = out + up[:, k:k + L] * w[k]
    return out


def grid_pos_embed(L):
    rows = L // GRID_W
    row = jnp.repeat(jnp.arange(rows, dtype=F32), GRID_W)
    col = jnp.tile(jnp.arange(GRID_W, dtype=F32), rows)
    quarter = D_MODEL // 4
    omega = POS_BASE ** (-jnp.arange(quarter, dtype=F32) / quarter)
    def emb(p):
        ang = p[:, None] * omega[None]
        return jnp.concatenate([jnp.sin(ang), jnp.cos(ang)], axis=-1)
    return jnp.concatenate([emb(row), emb(col)], axis=-1)


def ffn_sublayer(x, shift, scale, gate, g_pre, g_post, w1, w3, w2):
    h = modulate(rmsnorm(x, g_pre), shift, scale)
    y = (jax.nn.silu(h @ w1) * (h @ w3)) @ w2
    return x + MACARON * gate * rmsnorm(y, g_post)


def hyena_filters(L, fw0, fb0, fw1, fb1, fw2, fb2, freq, fwout):
    t = jnp.linspace(0.0, 1.0, L, dtype=F32)[:, None]
    w = 2.0 * math.pi * jnp.arange(L, dtype=F32)[:, None] / L
    f = jnp.linspace(1e-4, HY_BANDS - 1, HY_BANDS, dtype=F32)[None]
    phase = f * w
    z = jnp.concatenate([t, jnp.cos(phase), -jnp.sin(phase)], axis=-1)
    hdn = jnp.sin(freq[0] * (z @ fw0 + fb0))
    hdn = jnp.sin(freq[1] * (hdn @ fw1 + fb1))
    hdn = jnp.sin(freq[2] * (hdn @ fw2 + fb2))
    filt = (hdn @ fwout).astype(F32)
    max_decay = math.log(HY_DECAY_TARGET) / HY_FAST_DECAY
    min_decay = math.log(HY_DECAY_TARGET) / HY_SLOW_DECAY
    deltas = jnp.abs(jnp.linspace(min_decay, max_decay, HY_WIDTH, dtype=F32))
    decay = jnp.exp(-t * deltas[None])
    return filt[:, :HY_WIDTH] * decay, filt[:, HY_WIDTH:] * decay


def bidir_long_conv(u, h_fwd, h_bwd, bias):
    L = u.shape[1]
    filt_circ = jnp.concatenate([h_fwd, jnp.zeros_like(h_fwd[:1]), h_bwd[:0:-1]], axis=0)
    k_f = jnp.fft.rfft(filt_circ, n=2 * L, axis=0)
    uf = u.astype(F32)
    u_f = jnp.fft.rfft(uf, n=2 * L, axis=1)
    y = jnp.fft.irfft(u_f * k_f[None], n=2 * L, axis=1)[:, :L]
    return (y + uf * bias.astype(F32)).astype(u.dtype)


def hyena_mixer(h, w_in, b_in, conv_w, conv_b, fw0, fb0, fw1, fb1, fw2, fb2, freq, fwout,
                filt_bias, w_out, b_out):
    L = h.shape[1]
    u = dwconv(h @ w_in + b_in, conv_w, conv_b)
    x0, x1, v = jnp.split(u, 3, axis=-1)
    h_fwd, h_bwd = hyena_filters(L, fw0, fb0, fw1, fb1, fw2, fb2, freq, fwout)
    y = x0 * bidir_long_conv(x1 * v, h_fwd, h_bwd, filt_bias)
    return y @ w_out + b_out


def rglru_coeffs(xc, wa, ba, wi, bi, lam):
    B, L, R = xc.shape
    xh = xc.reshape(B, L, RG_HEADS, RG_BLOCK)
    r = jax.nn.sigmoid((jnp.einsum('blhi,hij->blhj', xh, wa).reshape(B, L, R) + ba).astype(F32))
    i = jax.nn.sigmoid((jnp.einsum('blhi,hij->blhj', xh, wi).reshape(B, L, R) + bi).astype(F32))
    log_a = -RG_C * r * jax.nn.softplus(-lam.astype(F32))
    a = jnp.exp(log_a)
    b = jnp.sqrt(-jnp.expm1(2.0 * log_a)) * i * xc.astype(F32)
    return a, b


def linear_scan(a, b, h0, reverse):
    if h0 is not None:
        first = -1 if reverse else 0
        b = b.at[:, first].add(a[:, first] * h0)
    def combine(left, right):
        a_l, b_l = left
        a_r, b_r = right
        return a_r * a_l, a_r * b_l + b_r
    _, hs = lax.associative_scan(combine, (a, b), axis=1, reverse=reverse)
    return hs


def rglru_mixer(h_c, h_l, ctx_out, w_in, b_in, conv_w, conv_b, wa, ba, wi, bi, lam, w_out, b_out):
    def project(h):
        u = h @ w_in + b_in
        gate_br, rec_br = jnp.split(u, 2, axis=-1)
        return gate_br, dwconv(rec_br, conv_w, conv_b)
    g_c, xc_c = project(h_c)
    g_l, xc_l = project(h_l)
    y_l = 0.0
    y_c = 0.0
    for d, rev in enumerate((False, True)):
        a_c, b_c = rglru_coeffs(xc_c, wa[d], ba[d], wi[d], bi[d], lam[d])
        hs_c = linear_scan(a_c, b_c, None, rev)
        h_end = hs_c[:, 0] if rev else hs_c[:, -1]
        a_l, b_l = rglru_coeffs(xc_l, wa[d], ba[d], wi[d], bi[d], lam[d])
        y_l = y_l + linear_scan(a_l, b_l, h_end, rev)
        if ctx_out:
            y_c = y_c + hs_c
    out_l = ((y_l * jax.nn.gelu(g_l)) @ w_out + b_out).astype(h_l.dtype)
    out_c = ((y_c * jax.nn.gelu(g_c)) @ w_out + b_out).astype(h_c.dtype) if ctx_out else None
    return out_l, out_c


def setup_inputs(seed: int = 0) -> dict:
    key = jax.random.key(seed)
    ks = iter(jax.random.split(key, 64))
    def nrm(shape, scale):
        return jax.random.normal(next(ks), shape, F32) * scale
    D, F, R = D_MODEL, D_FF, RG_WIDTH
    H3 = 3 * HY_WIDTH
    FO = HY_FILTER_HIDDEN
    a_init = jax.random.uniform(next(ks), (N_RGLRU, 2, R), F32, 0.9, 0.999) ** (1.0 / RG_C)
    return {
        "x": nrm((BATCH, SEQ, D), 1.0),
        "c": nrm((BATCH, D), 1.0),
        "ctx": nrm((BATCH, CTX_LEN, D), 1.0),
        "c_ctx": nrm((D,), 1.0),
        "ada_w": nrm((DEPTH, D, N_MOD * D), D ** -0.5),
        "ada_b": nrm((DEPTH, N_MOD * D), 0.01),
        "norm_g": 1.0 + nrm((DEPTH, 6, D), 0.05),
        "ffn_w1": nrm((DEPTH, 2, D, F), D ** -0.5),
        "ffn_w3": nrm((DEPTH, 2, D, F), D ** -0.5),
        "ffn_w2": nrm((DEPTH, 2, F, D), F ** -0.5),
        "hy_w_in": nrm((N_HYENA, D, H3), D ** -0.5),
        "hy_b_in": nrm((N_HYENA, H3), 0.01),
        "hy_conv_w": nrm((N_HYENA, HY_SHORT, H3), HY_SHORT ** -0.5),
        "hy_conv_b": nrm((N_HYENA, H3), 0.01),
        "hy_fw0": nrm((N_HYENA, HY_EMB, FO), HY_EMB ** -0.5),
        "hy_fb0": nrm((N_HYENA, FO), 0.1),
        "hy_fw1": nrm((N_HYENA, FO, FO), FO ** -0.5),
        "hy_fb1": nrm((N_HYENA, FO), 0.1),
        "hy_fw2": nrm((N_HYENA, FO, FO), FO ** -0.5),
        "hy_fb2": nrm((N_HYENA, FO), 0.1),
        "hy_freq": 1.0 + nrm((N_HYENA, 3, FO), 0.05),
        "hy_fwout": nrm((N_HYENA, FO, 2 * HY_WIDTH), FO ** -0.5),
        "hy_filt_bias": nrm((N_HYENA, HY_WIDTH), 1.0),
        "hy_w_out": nrm((N_HYENA, HY_WIDTH, D), HY_WIDTH ** -0.5),
        "hy_b_out": nrm((N_HYENA, D), 0.01),
        "rg_w_in": nrm((N_RGLRU, D, 2 * R), D ** -0.5),
        "rg_b_in": nrm((N_RGLRU, 2 * R), 0.01),
        "rg_conv_w": nrm((N_RGLRU, RG_CONV, R), RG_CONV ** -0.5),
        "rg_conv_b": nrm((N_RGLRU, R), 0.01),
        "rg_wa": nrm((N_RGLRU, 2, RG_HEADS, RG_BLOCK, RG_BLOCK), RG_BLOCK ** -0.5),
        "rg_ba": nrm((N_RGLRU, 2, R), 0.01),
        "rg_wi": nrm((N_RGLRU, 2, RG_HEADS, RG_BLOCK, RG_BLOCK), RG_BLOCK ** -0.5),
        "rg_bi": nrm((N_RGLRU, 2, R), 0.01),
        "rg_lam": jnp.log(a_init) - jnp.log1p(-a_init),
        "rg_w_out": nrm((N_RGLRU, R, D), R ** -0.5),
        "rg_b_out": nrm((N_RGLRU, D), 0.01),
    }


def reference(x, c, ctx, c_ctx, ada_w, ada_b, norm_g, ffn_w1, ffn_w3, ffn_w2,
              hy_w_in, hy_b_in, hy_conv_w, hy_conv_b, hy_fw0, hy_fb0, hy_fw1, hy_fb1,
              hy_fw2, hy_fb2, hy_freq, hy_fwout, hy_filt_bias, hy_w_out, hy_b_out,
              rg_w_in, rg_b_in, rg_conv_w, rg_conv_b, rg_wa, rg_ba, rg_wi, rg_bi, rg_lam,
              rg_w_out, rg_b_out):
    L = x.shape[1]
    x = x + grid_pos_embed(L).astype(x.dtype)[None]
    s = ctx
    for i in range(DEPTH):
        kind = i % N_MIXERS
        j = i // N_MIXERS
        last = i == DEPTH - 1
        ctx_out = not last
        ctx_in = ctx_out or kind == 1
        g = norm_g[i]
        mod_l = jnp.split((jax.nn.silu(c) @ ada_w[i] + ada_b[i])[:, None, :], N_MOD, axis=-1)
        x = ffn_sublayer(x, mod_l[0], mod_l[1], mod_l[2], g[0], g[1],
                         ffn_w1[i, 0], ffn_w3[i, 0], ffn_w2[i, 0])
        if ctx_in:
            mod_c = jnp.split(jax.nn.silu(c_ctx) @ ada_w[i] + ada_b[i], N_MOD, axis=-1)
            s = ffn_sublayer(s, mod_c[0], mod_c[1], mod_c[2], g[0], g[1],
                             ffn_w1[i, 0], ffn_w3[i, 0], ffn_w2[i, 0])
            h_c = modulate(rmsnorm(s, g[2]), mod_c[3], mod_c[4])
        h_l = modulate(rmsnorm(x, g[2]), mod_l[3], mod_l[4])
        if kind == 0:
            hp = (hy_w_in[j], hy_b_in[j], hy_conv_w[j], hy_conv_b[j], hy_fw0[j], hy_fb0[j],
                  hy_fw1[j], hy_fb1[j], hy_fw2[j], hy_fb2[j], hy_freq[j], hy_fwout[j],
                  hy_filt_bias[j], hy_w_out[j], hy_b_out[j])
            y_l = hyena_mixer(h_l, *hp)
            y_c = hyena_mixer(h_c, *hp) if ctx_out else None
        else:
            y_l, y_c = rglru_mixer(h_c, h_l, ctx_out, rg_w_in[j], rg_b_in[j], rg_conv_w[j],
                                   rg_conv_b[j], rg_wa[j], rg_ba[j], rg_wi[j], rg_bi[j],
                                   rg_lam[j], rg_w_out[j], rg_b_out[j])
        x = x + mod_l[5] * rmsnorm(y_l, g[3])
        x = ffn_sublayer(x, mod_l[6], mod_l[7], mod_l[8], g[4], g[5],
                         ffn_w1[i, 1], ffn_w3[i, 1], ffn_w2[i, 1])
        if ctx_out:
            s = s + mod_c[5] * rmsnorm(y_c, g[3])
            s = ffn_sublayer(s, mod_c[6], mod_c[7], mod_c[8], g[4], g[5],
                             ffn_w1[i, 1], ffn_w3[i, 1], ffn_w2[i, 1])
    return x
```

```python
import math
from contextlib import ExitStack

import numpy as np
import ml_dtypes
import concourse.bass as bass
import concourse.mybir as mybir
from concourse.bass_utils import run_bass_kernel_spmd

F32 = mybir.dt.float32
BF16 = mybir.dt.bfloat16
AF = mybir.ActivationFunctionType
ALU = mybir.AluOpType

D = 1024
DFF = 2816
NFC = 22
LAT = 2048
CTX = 256
NT = LAT + CTX
EPS = 1e-6
STAGES = 99

ENGS = ("pe", "act", "dve", "pool", "sp")


class Op:
    __slots__ = ("eng", "emit", "deps", "dma", "sem", "val", "signal")

    def __init__(self, eng, emit, dma):
        self.eng = eng
        self.emit = emit
        self.deps = []
        self.dma = dma
        self.sem = None
        self.val = None
        self.signal = False


class Sched:
    def __init__(self, nc, n_dma_sems=16):
        self.nc = nc
        self.ops = []
        self.last_w = {}
        self.readers = {}
        self.n_dma_sems = n_dma_sems
        self.bar_pos = 0

    def op(self, eng, emit, reads=(), writes=(), dma=False):
        o = Op(eng, emit, dma)
        deps = {}
        for r in reads:
            w = self.last_w.get(r)
            if w is not None:
                deps[id(w)] = w
        for r in writes:
            w = self.last_w.get(r)
            if w is not None:
                deps[id(w)] = w
            for rd in self.readers.get(r, ()):
                deps[id(rd)] = rd
        for d in deps.values():
            if d.eng == "pe" and eng == "pe" and not d.dma and not dma:
                continue
            o.deps.append(d)
            d.signal = True
        for r in reads:
            self.readers.setdefault(r, []).append(o)
        for r in writes:
            self.last_w[r] = o
            self.readers[r] = []
        self.ops.append(o)
        return o

    def barrier(self):
        lasts = {}
        for o in self.ops:
            if o.emit is not None and not o.dma:
                lasts[o.eng] = o
        dmas = [o for o in self.ops[self.bar_pos:] if o.dma]
        self.bar_pos = len(self.ops)
        for e in ENGS:
            b = Op(e, None, False)
            b.deps = list(lasts.values()) + dmas
            for d in b.deps:
                d.signal = True
            self.ops.append(b)

    def finalize(self, final_wait_ops=()):
        nc = self.nc
        for o in final_wait_ops:
            o.signal = True
        with ExitStack() as es:
            esem = {e: es.enter_context(nc.semaphore("s_" + e)) for e in ENGS}
            dsem = {e: [es.enter_context(nc.semaphore("d_%s_%d" % (e, i)))
                        for i in range(self.n_dma_sems)] for e in ("sp", "pool", "act")}
            ecount = {e: 0 for e in ENGS}
            dcount = {e: [0] * self.n_dma_sems for e in dsem}
            drr = {e: 0 for e in dsem}
            prev_on_sem = {}
            for o in self.ops:
                if o.dma:
                    i = drr[o.eng]
                    drr[o.eng] = (i + 1) % self.n_dma_sems
                    dcount[o.eng][i] += 16
                    o.sem = dsem[o.eng][i]
                    o.val = dcount[o.eng][i]
                    p = prev_on_sem.get((o.eng, i))
                    if p is not None:
                        o.deps.append(p)
                    prev_on_sem[(o.eng, i)] = o
                elif o.signal:
                    ecount[o.eng] += 1
                    o.sem = esem[o.eng]
                    o.val = ecount[o.eng]
            per_eng = {e: [o for o in self.ops if o.eng == e] for e in ENGS}
            final_waits = [(o.sem, o.val) for o in final_wait_ops]
            with nc.Block() as block:
                def run(eng_name, eng):
                    seen = {}
                    for o in per_eng[eng_name]:
                        need = {}
                        for d in o.deps:
                            k = id(d.sem)
                            if seen.get(k, 0) >= d.val:
                                continue
                            if k not in need or need[k][1] < d.val:
                                need[k] = (d.sem, d.val)
                        for k, (s, v) in need.items():
                            eng.wait_ge(s, v)
                            seen[k] = v
                        if o.emit is None:
                            continue
                        inst = o.emit(eng)
                        if o.dma:
                            inst.then_inc(o.sem, 16)
                        elif o.signal:
                            inst.then_inc(o.sem, 1)
                    if eng_name == "sp":
                        for (s, v) in final_waits:
                            eng.wait_ge(s, v)

                @block.tensor
                def _(e):
                    run("pe", e)

                @block.scalar
                def _(e):
                    run("act", e)

                @block.vector
                def _(e):
                    run("dve", e)

                @block.gpsimd
                def _(e):
                    run("pool", e)

                @block.sync
                def _(e):
                    run("sp", e)


V_ADAB = 0
V_NG = 144
V_HBIN = 240
V_HCW = 264
V_HCB = 336
V_HBOUT = 360
V_RBIN = 368
V_RCW = 384
V_RCB = 416
V_RBA = 424
V_RBI = 440
V_RLAM = 456
V_RBOUT = 472
NV = 480

PADW = 2310
TT = [(0, 256)] + [(256 + 512 * i, 256 + 512 * (i + 1)) for i in range(4)]
FT = [(256 * i, 256 * (i + 1)) for i in range(9)]


def poff(t):
    return t + 2 if t < CTX else t + 4


def build_program(stages=STAGES):
    nc = bass.Bass("TRN2", target_bir_lowering=False)

    def din(name, shape, dt=F32):
        return nc.dram_tensor(name, list(shape), dt, kind="ExternalInput").ap()

    xin = din("xin", [128, 8, LAT])
    cin = din("cin", [128, 8, CTX])
    pos = din("pos", [128, 8, LAT])
    cvec = din("cvec", [128, 8, 2])
    ada_w = din("ada_w", [2, D, 9 * D])
    vecs_d = din("vecs", [128, NV])
    w1_d = din("ffn_w1", [2, 2, D, DFF])
    w3_d = din("ffn_w3", [2, 2, D, DFF])
    w2_d = din("ffn_w2", [2, 2, DFF, D])
    hy_w_in = din("hy_w_in", [D, 3 * D])
    hy_w_out = din("hy_w_out", [D, D])
    rg_w_in = din("rg_w_in", [D, 2 * D])
    rg_w_out = din("rg_w_out", [D, D])
    rg_wa = din("rg_wa", [2, 4, 256, 256])
    rg_wi = din("rg_wi", [2, 4, 256, 256])
    zT_d = din("zT", [33, NT])
    fw0_d = din("fw0", [33, 64])
    fw1_d = din("fw1", [64, 64])
    fw2_d = din("fw2", [64, 64])
    fvec_d = din("fvec", [64, 6])
    fwout_d = din("fwout", [64, 2 * D])
    delta_d = din("delta_b", [128, D])
    negt_d = din("negt", [128, 18])
    fbias_d = din("fbias_b", [128, D])
    dftF = din("dftF", [16, 2, 128, 16, 128], BF16)
    dftFc = din("dftFc", [2, 2, 128, 2, 128], BF16)
    dftI = din("dftI", [4, 16, 128, 2, 512], BF16)
    dftIc = din("dftIc", [2, 128, 2, 256], BF16)
    out_d = nc.dram_tensor("out", [128, 8, LAT], F32, kind="ExternalOutput").ap()
    zs = nc.dram_tensor("zs", [128, 8, NT], BF16, kind="ExternalOutput").ap()

    S = Sched(nc)
    OP = S.op
    out_ops = []

    with ExitStack() as top:
        uniq = [0]

        def sbt(es, name, shape, dt):
            uniq[0] += 1
            return es.enter_context(nc.sbuf_tensor("sb%d_%s" % (uniq[0], name), list(shape), dt))

        PS = top.enter_context(nc.psum_tensor("ps", [128, 8, 512], F32))
        xT = sbt(top, "xT", [128, 8, NT], F32)
        vecs = sbt(top, "vecs", [128, NV], F32)
        tab = sbt(top, "tab", [128, 2, 9, 8], F32)
        ones = sbt(top, "ones", [128, 128], BF16)
        ident = sbt(top, "ident", [128, 128], BF16)
        identf = sbt(top, "identf", [128, 128], F32)
        softp = sbt(top, "softp", [128, 16], F32)

        bank_rr = [0]

        def next_bank(lo=0, hi=8):
            b = lo + bank_rr[0] % (hi - lo)
            bank_rr[0] += 1
            return b

        def BK(b):
            return "bank%d" % b

        OP("sp", lambda e: e.dma_start(out=vecs[:], in_=vecs_d), writes=["vecs"], dma=True)
        OP("pool", lambda e: e.memset(ones[:], 1.0), writes=["ones"])
        OP("pool", lambda e: e.memset(identf[:], 0.0), writes=["identf"])
        OP("pool", lambda e: e.affine_select(out=identf[:], in_=identf[:], pattern=[[-1, 128]],
                                             compare_op=ALU.not_equal, fill=1.0, base=0,
                                             channel_multiplier=1), writes=["identf"])
        OP("dve", lambda e: e.tensor_copy(ident[:], identf[:]), reads=["identf"], writes=["ident"])

        def xres(c, t0):
            return "x%d_%d" % (c, t0 // 256)

        def xres_range(c, t0, t1):
            return [xres(c, t) for t in range(t0, t1, 256)]

        with ExitStack() as ph:
            ptile = sbt(ph, "ptile", [128, 2, 8, 256], F32)
            OP("sp", lambda e: e.dma_start(out=xT[:, :, 0:CTX], in_=cin),
               writes=[xres(c, 0) for c in range(8)], dma=True)
            for c in range(8):
                OP("sp", lambda e, c=c: e.dma_start(out=xT[:, c, CTX:NT], in_=xin[:, c, :]),
                   writes=xres_range(c, CTX, NT), dma=True)
            for i in range(8):
                t0 = i * 256
                OP("sp", lambda e, i=i, t0=t0: e.dma_start(out=ptile[:, i % 2, :, :], in_=pos[:, :, t0:t0 + 256]),
                   writes=["ptile%d" % (i % 2)], dma=True)
                eng = "dve" if i % 2 == 0 else "pool"
                OP(eng, lambda e, i=i, t0=t0: e.tensor_tensor(
                    xT[:, :, CTX + t0:CTX + t0 + 256], xT[:, :, CTX + t0:CTX + t0 + 256],
                    ptile[:, i % 2, :, :], ALU.add),
                   reads=["ptile%d" % (i % 2)], writes=[xres(c, CTX + t0) for c in range(8)])

        def ada_layer(l):
            S.barrier()
            with ExitStack() as ph:
                aw = sbt(ph, "aw", [128, 2, 8, 512], F32)
                sc = sbt(ph, "sc", [128, 8, 2], F32)
                modv = sbt(ph, "modv", [128, 2, 72], F32)
                OP("sp", lambda e: e.dma_start(out=sc[:], in_=cvec), writes=["sc"], dma=True)
                OP("act", lambda e: e.activation(sc[:], sc[:], AF.Silu), reads=["sc"], writes=["sc"])
                awv = ada_w[l].rearrange("(kc p) m -> p kc m", p=128)
                bank = 7
                for i in range(18):
                    OP("sp", lambda e, i=i: e.dma_start(out=aw[:, i % 2, :, :], in_=awv[:, :, i * 512:(i + 1) * 512]),
                       writes=["aw%d" % (i % 2)], dma=True)

                    def mm(e, i=i):
                        r = None
                        for m in range(4):
                            mc = i * 4 + m
                            for kc in range(8):
                                r = e.matmul(PS[:, bank, 2 * mc:2 * mc + 2], aw[:, i % 2, kc, m * 128:(m + 1) * 128],
                                             sc[:, kc, :], start=(kc == 0), stop=(kc == 7), skip_group_check=True)
                        return r
                    OP("pe", mm, reads=["aw%d" % (i % 2), "sc"], writes=[BK(bank)])
                for s in range(2):
                    OP("dve", lambda e, s=s: e.tensor_tensor(modv[:, s, :], PS[:, bank, s:144:2],
                                                            vecs[:, V_ADAB + l * 72:V_ADAB + (l + 1) * 72], ALU.add),
                       reads=["vecs"], writes=[BK(bank), "modv"])

                def g(i):
                    return vecs[:, V_NG + (l * 6 + i) * 8:V_NG + (l * 6 + i) * 8 + 8]

                def mo(s, i):
                    return modv[:, s, i * 8:(i + 1) * 8]
                for s in range(2):
                    for sub in range(3):
                        gpre, gpost = g(2 * sub), g(2 * sub + 1)
                        mac = 0.5 if sub != 1 else 1.0
                        OP("dve", lambda e, s=s, sub=sub, gpre=gpre: e.scalar_tensor_tensor(
                            tab[:, s, 3 * sub, :], mo(s, 3 * sub + 1), 1.0, gpre, ALU.add, ALU.mult),
                           reads=["modv", "vecs"], writes=["tab"])
                        OP("dve", lambda e, s=s, sub=sub: e.tensor_copy(tab[:, s, 3 * sub + 1, :], mo(s, 3 * sub)),
                           reads=["modv"], writes=["tab"])
                        OP("dve", lambda e, s=s, sub=sub, gpost=gpost, mac=mac: e.scalar_tensor_tensor(
                            tab[:, s, 3 * sub + 2, :], mo(s, 3 * sub + 2), mac, gpost, ALU.mult, ALU.mult),
                           reads=["modv", "vecs"], writes=["tab"])

        def TAB(s, q, c):
            return tab[:, s, q, c:c + 1]

        def prenorm_tile(t0, t1, s, qA, sq, sd, rstd, tmp, hdst, hres):
            T = t1 - t0
            bank = 7
            OP("act", lambda e: e.activation(sq[:, :, 0:T], xT[:, :, t0:t1], AF.Square),
               reads=[xres(c, t0) for c in range(8)], writes=["sq%d" % c for c in range(8)])

            def mm(e):
                r = None
                for c in range(8):
                    r = e.matmul(PS[:, bank, 0:T], ones[:], sq[:, c, 0:T], start=(c == 0), stop=(c == 7))
                return r
            OP("pe", mm, reads=["sq%d" % c for c in range(8)] + ["ones"], writes=[BK(bank)])
            OP("act", lambda e: e.activation(sd[:, 0:T], PS[:, bank, 0:T], AF.Sqrt, bias=EPS, scale=1.0 / D),
               writes=[BK(bank), "sd"])
            OP("dve", lambda e: e.reciprocal(rstd[:, 0:T], sd[:, 0:T]), reads=["sd"], writes=["rstd"])
            for c in range(8):
                OP("dve", lambda e, c=c: e.scalar_tensor_tensor(tmp[:, c % 2, 0:T], xT[:, c, t0:t1], TAB(s, qA, c),
                                                               rstd[:, 0:T], ALU.mult, ALU.mult),
                   reads=[xres(c, t0), "rstd", "tab"], writes=["tmp%d" % (c % 2)])
                OP("act", lambda e, c=c: e.activation(hdst(c), tmp[:, c % 2, 0:T], AF.Identity,
                                                     bias=TAB(s, qA + 1, c), scale=1.0),
                   reads=["tmp%d" % (c % 2), "tab"], writes=[hres])

        def postnorm_tile(t0, t1, s, qC, ysrc, yres, sq, sd, rstd, tmp):
            T = t1 - t0
            bank = 7
            for dc in range(8):
                OP("act", lambda e, dc=dc: e.activation(sq[:, dc, 0:T], ysrc(dc), AF.Square),
                   writes=[yres(dc), "sq%d" % dc])

            def mm(e):
                r = None
                for c in range(8):
                    r = e.matmul(PS[:, bank, 0:T], ones[:], sq[:, c, 0:T], start=(c == 0), stop=(c == 7))
                return r
            OP("pe", mm, reads=["sq%d" % dc for dc in range(8)] + ["ones"], writes=[BK(bank)])
            OP("act", lambda e: e.activation(sd[:, 0:T], PS[:, bank, 0:T], AF.Sqrt, bias=EPS, scale=1.0 / D),
               writes=[BK(bank), "sd"])
            OP("dve", lambda e: e.reciprocal(rstd[:, 0:T], sd[:, 0:T]), reads=["sd"], writes=["rstd"])
            for dc in range(8):
                OP("dve", lambda e, dc=dc: e.scalar_tensor_tensor(tmp[:, dc % 2, 0:T], ysrc(dc), TAB(s, qC, dc),
                                                                 rstd[:, 0:T], ALU.mult, ALU.mult),
                   reads=["rstd", "tab"], writes=[yres(dc), "tmp%d" % (dc % 2)])
                OP("pool", lambda e, dc=dc: e.tensor_tensor(xT[:, dc, t0:t1], xT[:, dc, t0:t1],
                                                           tmp[:, dc % 2, 0:T], ALU.add),
                   reads=["tmp%d" % (dc % 2)], writes=[xres(dc, t0)])

        def ffn(l, k, tiles):
            fi = l * 2 + k
            q0 = 0 if k == 0 else 6
            S.barrier()
            with ExitStack() as ph:
                W1 = sbt(ph, "W1", [128, 8, DFF], BF16)
                W3 = sbt(ph, "W3", [128, 8, DFF], BF16)
                w2r = sbt(ph, "w2r", [128, 3, 2, D], BF16)
                hbuf = sbt(ph, "hbuf", [128, 2, 8, 256], BF16)
                hid = sbt(ph, "hid", [128, 4, 256], BF16)
                sa = sbt(ph, "sa", [128, 2, 256], F32)
                sq = sbt(ph, "sq", [128, 8, 256], BF16)
                sd = sbt(ph, "sd", [128, 256], F32)
                rstd = sbt(ph, "rstd", [128, 256], F32)
                tmp = sbt(ph, "tmp", [128, 2, 256], F32)
                w1v = w1_d[l, k].rearrange("(kc p) f -> p kc f", p=128)
                w3v = w3_d[l, k].rearrange("(kc p) f -> p kc f", p=128)
                for j in range(11):
                    OP("pool", lambda e, j=j: e.dma_start(out=W1[:, :, j * 256:(j + 1) * 256], in_=w1v[:, :, j * 256:(j + 1) * 256]),
                       writes=["W1_%d" % j], dma=True)
                    OP("pool", lambda e, j=j: e.dma_start(out=W3[:, :, j * 256:(j + 1) * 256], in_=w3v[:, :, j * 256:(j + 1) * 256]),
                       writes=["W3_%d" % j], dma=True)
                w2v = w2_d[l, k].rearrange("(fc p) d -> p fc d", p=128)
                npieces = len(tiles) * 11

                def w2_load(gp):
                    if gp >= npieces:
                        return
                    j = gp % 11
                    OP("pool", lambda e, gp=gp, j=j: e.dma_start(out=w2r[:, gp % 3, :, :], in_=w2v[:, 2 * j:2 * j + 2, :]),
                       writes=["w2r%d" % (gp % 3)], dma=True)
                w2_load(0)
                w2_load(1)

                def do_prenorm(ti):
                    t0, t1 = FT[tiles[ti]]
                    s = 1 if tiles[ti] == 0 else 0
                    prenorm_tile(t0, t1, s, q0, sq, sd, rstd, tmp,
                                 lambda c, ti=ti: hbuf[:, ti % 2, c, :], "hbuf%d" % (ti % 2))
                do_prenorm(0)
                ring = [0]
                hcount = [0]
                for ti in range(len(tiles)):
                    t0, t1 = FT[tiles[ti]]
                    s = 1 if tiles[ti] == 0 else 0
                    if ti + 1 < len(tiles):
                        do_prenorm(ti + 1)
                    hb = ti % 2
                    slots = {}

                    def hid_stage(f, hb=hb):
                        ba = 4 + ring[0] % 3
                        bb = 4 + (ring[0] + 1) % 3
                        ring[0] += 2
                        hs = hcount[0] % 4
                        hcount[0] += 1
                        slots[f] = hs
                        j = f // 2

                        def mma(e):
                            r = None
                            for kc in range(8):
                                r = e.matmul(PS[:, ba, 0:256], W1[:, kc, f * 128:(f + 1) * 128], hbuf[:, hb, kc, :],
                                             start=(kc == 0), stop=(kc == 7))
                            return r

                        def mmb(e):
                            r = None
                            for kc in range(8):
                                r = e.matmul(PS[:, bb, 0:256], W3[:, kc, f * 128:(f + 1) * 128], hbuf[:, hb, kc, :],
                                             start=(kc == 0), stop=(kc == 7))
                            return r
                        OP("pe", mma, reads=["W1_%d" % j, "hbuf%d" % hb], writes=[BK(ba)])
                        OP("pe", mmb, reads=["W3_%d" % j, "hbuf%d" % hb], writes=[BK(bb)])
                        OP("act", lambda e: e.activation(sa[:, f % 2, :], PS[:, ba, 0:256], AF.Silu),
                           writes=[BK(ba), "sa%d" % (f % 2)])
                        OP("dve", lambda e: e.tensor_tensor(hid[:, hs, :], sa[:, f % 2, :], PS[:, bb, 0:256], ALU.mult),
                           reads=["sa%d" % (f % 2)], writes=[BK(bb), "hid%d" % hs])

                    def w2_stage(f):
                        gp = ti * 11 + f // 2
                        if f % 2 == 0:
                            w2_load(gp + 2)
                        hs = slots[f]

                        def mm(e):
                            r = None
                            for dc in range(8):
                                r = e.matmul(PS[:, dc // 2, (dc % 2) * 256:(dc % 2) * 256 + 256],
                                             w2r[:, gp % 3, f % 2, dc * 128:(dc + 1) * 128], hid[:, hs, :],
                                             start=(f == 0 and dc % 2 == 0), stop=(f == NFC - 1),
                                             skip_group_check=True)
                            return r
                        OP("pe", mm, reads=["w2r%d" % (gp % 3), "hid%d" % hs], writes=[BK(b) for b in range(4)])
                    for f in range(NFC + 1):
                        if f < NFC:
                            hid_stage(f)
                        if f >= 1:
                            w2_stage(f - 1)
                    postnorm_tile(t0, t1, s, q0 + 2,
                                  lambda dc: PS[:, dc // 2, (dc % 2) * 256:(dc % 2) * 256 + 256],
                                  lambda dc: BK(dc // 2), sq, sd, rstd, tmp)

        def linear_rows(wsrc, wres, src, srcres, dst, dstres, bias_ap, tiles=TT, func=AF.Identity):
            for (t0, t1) in tiles:
                T = t1 - t0
                b = next_bank(0, 7)

                def mm(e, t0=t0, t1=t1, b=b, T=T):
                    r = None
                    for kc in range(8):
                        r = e.matmul(PS[:, b, 0:T], wsrc(kc), src[:, kc, t0:t1], start=(kc == 0), stop=(kc == 7))
                    return r
                OP("pe", mm, reads=[wres, srcres], writes=[BK(b)])
                OP("act", lambda e, t0=t0, t1=t1, b=b, T=T: e.activation(dst(t0, t1), PS[:, b, 0:T], func,
                                                                        bias=bias_ap, scale=1.0),
                   reads=["vecs"], writes=[BK(b), dstres])

        def conv_rows(upad, acc, cw, cb, K, dst_segs, eng):
            eng = "dve"
            n = PADW - 4
            OP(eng, lambda e: e.tensor_scalar(acc[:, 2:2 + n], upad[:, 1:1 + n], cw(0), cb, ALU.mult, ALU.add),
               reads=["upad", "vecs"], writes=["acc"])
            for k in range(1, K - 1):
                OP(eng, lambda e, k=k: e.scalar_tensor_tensor(acc[:, 2:2 + n], upad[:, 1 + k:1 + k + n], cw(k),
                                                             acc[:, 2:2 + n], ALU.mult, ALU.add),
                   reads=["upad", "vecs", "acc"], writes=["acc"])
            k = K - 1
            for (dst, dres, p0, ln) in dst_segs:
                OP(eng, lambda e, dst=dst, p0=p0, ln=ln: e.scalar_tensor_tensor(
                    dst, upad[:, p0 + k - 1:p0 + k - 1 + ln], cw(k), acc[:, p0:p0 + ln], ALU.mult, ALU.add),
                   reads=["upad", "vecs", "acc"], writes=[dres])

        ones_f32 = sbt(top, "ones_f32", [64, 128], F32)
        OP("pool", lambda e: e.memset(ones_f32[:], 1.0), writes=["ones_f32"])
        SEGS = [(0, 2, CTX), (CTX, 4 + CTX, LAT)]

        def udst_of(upad):
            return lambda t0, t1: upad[:, poff(t0):poff(t0) + (t1 - t0)]

        def norm_bufs(es):
            return (sbt(es, "sq", [128, 8, 256], BF16), sbt(es, "sd", [128, 256], F32),
                    sbt(es, "rstd", [128, 256], F32), sbt(es, "tmp", [128, 2, 256], F32))

        def prenorm_all(hT, q):
            S.barrier()
            with ExitStack() as es:
                sq, sd, rstd, tmp = norm_bufs(es)
                for ti in range(9):
                    t0, t1 = FT[ti]
                    s = 1 if ti == 0 else 0
                    prenorm_tile(t0, t1, s, q, sq, sd, rstd, tmp, lambda c, t0=t0, t1=t1: hT[:, c, t0:t1], "hT")
                S.barrier()

        def mixer_out(w_out_d, zT, zres, bout_off, tiles):
            S.barrier()
            with ExitStack() as es:
                sq, sd, rstd, tmp = norm_bufs(es)
                wo = sbt(es, "wo", [128, 8, D], BF16)
                ysb = sbt(es, "ysb", [128, 8, 256], F32)
                wov = w_out_d.rearrange("(kc p) m -> p kc m", p=128)
                for h in range(2):
                    OP("pool", lambda e, h=h: e.dma_start(out=wo[:, :, h * 512:(h + 1) * 512], in_=wov[:, :, h * 512:(h + 1) * 512]),
                       writes=["wo%d" % h], dma=True)
                for ti in tiles:
                    t0, t1 = FT[ti]
                    s = 1 if ti == 0 else 0
                    for oc in range(8):
                        b = next_bank(0, 7)

                        def mm(e, oc=oc, b=b, t0=t0, t1=t1):
                            r = None
                            for kc in range(8):
                                r = e.matmul(PS[:, b, 0:256], wo[:, kc, oc * 128:(oc + 1) * 128], zT[:, kc, t0:t1],
                                             start=(kc == 0), stop=(kc == 7))
                            return r
                        OP("pe", mm, reads=["wo%d" % (oc // 4), zres], writes=[BK(b)])
                        OP("act", lambda e, oc=oc, b=b: e.activation(ysb[:, oc, :], PS[:, b, 0:256], AF.Identity,
                                                                    bias=vecs[:, bout_off + oc:bout_off + oc + 1], scale=1.0),
                           reads=["vecs"], writes=[BK(b), "ysb%d" % oc])
                    postnorm_tile(t0, t1, s, 5, lambda dc: ysb[:, dc, :], lambda dc: "ysb%d" % dc, sq, sd, rstd, tmp)
                S.barrier()

        def hyena(l):
            S.barrier()
            with ExitStack() as ph:
                hT = sbt(ph, "hT", [128, 8, NT], BF16)
                hdnT = sbt(ph, "hdnT", [64, NT], F32)
                prenorm_all(hT, 3)
                with ExitStack() as fp:
                    zT = sbt(fp, "zT", [33, NT], F32)
                    fw0 = sbt(fp, "fw0", [33, 64], F32)
                    fw1 = sbt(fp, "fw1", [64, 64], F32)
                    fw2 = sbt(fp, "fw2", [64, 64], F32)
                    fvec = sbt(fp, "fvec", [64, 6], F32)
                    arg = sbt(fp, "arg", [64, NT], F32)
                    wtmp = sbt(fp, "wtmp", [64, NT], F32)
                    h1 = sbt(fp, "h1", [64, NT], F32)
                    OP("sp", lambda e: e.dma_start(out=zT[:], in_=zT_d), writes=["zT"], dma=True)
                    OP("sp", lambda e: e.dma_start(out=fw0[:], in_=fw0_d), writes=["fw"], dma=True)
                    OP("sp", lambda e: e.dma_start(out=fw1[:], in_=fw1_d), writes=["fw"], dma=True)
                    OP("sp", lambda e: e.dma_start(out=fw2[:], in_=fw2_d), writes=["fw"], dma=True)
                    OP("sp", lambda e: e.dma_start(out=fvec[:], in_=fvec_d), writes=["fw"], dma=True)
                    layers = [(fw0, 33, zT, "zT", h1, "h1"), (fw1, 64, h1, "h1", hdnT, "hdnA"), (fw2, 64, hdnT, "hdnA", h1, "h1")]
                    for li, (fw, K, src, sres, dst, dres) in enumerate(layers):
                        for (t0, t1) in TT:
                            T = t1 - t0
                            b = next_bank(0, 7)
                            OP("pe", lambda e, fw=fw, K=K, src=src, b=b, t0=t0, t1=t1, T=T: e.matmul(
                                PS[0:64, b, 0:T], fw[0:K, :], src[0:K, t0:t1], start=True, stop=True),
                               reads=["fw", sres], writes=[BK(b)])
                            OP("dve", lambda e, b=b, t0=t0, t1=t1, T=T, li=li: e.tensor_scalar(
                                arg[:, t0:t1], PS[0:64, b, 0:T], fvec[:, li:li + 1], fvec[:, 3 + li:4 + li], ALU.add, ALU.mult),
                               reads=["fw"], writes=[BK(b), "arg"])
                        for rep in range(2):
                            OP("dve", lambda e: e.tensor_scalar(wtmp[:], arg[:], math.pi, -2 * math.pi, ALU.is_gt, ALU.mult),
                               reads=["arg"], writes=["wtmp"])
                            OP("dve", lambda e: e.tensor_tensor(arg[:], arg[:], wtmp[:], ALU.add), reads=["wtmp"], writes=["arg"])
                            OP("dve", lambda e: e.tensor_scalar(wtmp[:], arg[:], -math.pi, 2 * math.pi, ALU.is_lt, ALU.mult),
                               reads=["arg"], writes=["wtmp"])
                            OP("dve", lambda e: e.tensor_tensor(arg[:], arg[:], wtmp[:], ALU.add), reads=["wtmp"], writes=["arg"])
                        OP("act", lambda e, dst=dst: e.activation(dst[:], arg[:], AF.Sin, scale=1.0 - 2e-6),
                           reads=["arg"], writes=[dres])
                    OP("dve", lambda e: e.tensor_copy(hdnT[:], h1[:]), reads=["h1"], writes=["hdnT", "hdnA"])
                    S.barrier()
                with ExitStack() as cp:
                    prow = sbt(cp, "prow", [128, NT], BF16)
                    x0row = sbt(cp, "x0row", [128, NT], BF16)
                    zrow = sbt(cp, "zrow", [128, NT], BF16)
                    comb = sbt(cp, "comb", [128, 18, 384], BF16)
                    fwq = sbt(cp, "fwq", [64, 2, 128], F32)
                    hb0 = sbt(cp, "hb0", [64, 2, 128], F32)
                    kb = sbt(cp, "kb", [128, 2, 128], F32)
                    dl = sbt(cp, "dl", [128, 128], F32)
                    fbq = sbt(cp, "fbq", [128, 128], F32)
                    negt = sbt(cp, "negt", [128, 18], F32)
                    OP("sp", lambda e: e.dma_start(out=negt[:], in_=negt_d), writes=["negt"], dma=True)
                    wv = hy_w_in.rearrange("(kc p) m -> p kc m", p=128)
                    PT = PS[:, 7, :].bitcast(BF16)
                    fcount = [0]
                    icount = [0]
                    def hy_chunk(j):
                        OP("sp", lambda e, j=j: e.dma_start(out=fwq[:, 0, :], in_=fwout_d[:, j * 128:(j + 1) * 128]), writes=["fwq"], dma=True)
                        OP("sp", lambda e, j=j: e.dma_start(out=fwq[:, 1, :], in_=fwout_d[:, D + j * 128:D + (j + 1) * 128]), writes=["fwq"], dma=True)
                        OP("sp", lambda e, j=j: e.dma_start(out=dl[:], in_=delta_d[:, j * 128:(j + 1) * 128]), writes=["dl"], dma=True)
                        OP("sp", lambda e, j=j: e.dma_start(out=fbq[:], in_=fbias_d[:, j * 128:(j + 1) * 128]), writes=["fbq"], dma=True)
                        S.barrier()
                        with ExitStack() as sa_:
                            upad = sbt(sa_, "upad", [128, PADW], F32)
                            acc = sbt(sa_, "acc", [128, PADW], F32)
                            x1row = sbt(sa_, "x1row", [128, NT], BF16)
                            win = sbt(sa_, "win", [128, 3, 8, 128], BF16)
                            OP("pool", lambda e: e.memset(upad[:], 0.0), writes=["upad"])
                            for wi, oc in enumerate((8 + j, 16 + j, j)):
                                OP("pool", lambda e, wi=wi, oc=oc: e.dma_start(out=win[:, wi, :, :], in_=wv[:, :, oc * 128:(oc + 1) * 128]),
                                   writes=["win%d" % wi], dma=True)
                            udst = udst_of(upad)

                            def cwf(oc):
                                return lambda k: vecs[:, V_HCW + k * 24 + oc:V_HCW + k * 24 + oc + 1]
                            oc = 8 + j
                            linear_rows(lambda kc: win[:, 0, kc, :], "win0", hT, "hT", udst, "upad", vecs[:, V_HBIN + oc:V_HBIN + oc + 1])
                            conv_rows(upad, acc, cwf(oc), vecs[:, V_HCB + oc:V_HCB + oc + 1], 3,
                                      [(x1row[:, t:t + ln], "x1row", p, ln) for (t, p, ln) in SEGS], "dve")
                            oc = 16 + j
                            linear_rows(lambda kc: win[:, 1, kc, :], "win1", hT, "hT", udst, "upad", vecs[:, V_HBIN + oc:V_HBIN + oc + 1])
                            conv_rows(upad, acc, cwf(oc), vecs[:, V_HCB + oc:V_HCB + oc + 1], 3,
                                      [(x0row[:, t:t + ln], "x0row", p, ln) for (t, p, ln) in SEGS], "pool")
                            OP("dve", lambda e: e.tensor_tensor(prow[:], x1row[:], x0row[:], ALU.mult),
                               reads=["x1row", "x0row"], writes=["prow"])
                            for g0 in range(0, 18, 4):
                                gn = min(4, 18 - g0)

                                def tr(e, g0=g0, gn=gn):
                                    r = None
                                    for i in range(gn):
                                        r = e.transpose(PT[:, i * 128:(i + 1) * 128], prow[:, (g0 + i) * 128:(g0 + i + 1) * 128], ident[:])
                                    return r
                                OP("pe", tr, reads=["prow", "ident"], writes=[BK(7)])
                                for i in range(gn):
                                    OP("act", lambda e, g0=g0, i=i: e.copy(comb[:, g0 + i, 0:128], PT[:, i * 128:(i + 1) * 128]),
                                       writes=[BK(7), "comb"])
                            oc = j
                            linear_rows(lambda kc: win[:, 2, kc, :], "win2", hT, "hT", udst, "upad", vecs[:, V_HBIN + oc:V_HBIN + oc + 1])
                            conv_rows(upad, acc, cwf(oc), vecs[:, V_HCB + oc:V_HCB + oc + 1], 3,
                                      [(x0row[:, t:t + ln], "x0row", p, ln) for (t, p, ln) in SEGS], "dve")
                            S.barrier()
                        with ExitStack() as sb_:
                            Y = sbt(sb_, "Y", [128, 18, 2, 128], BF16)
                            Fr = sbt(sb_, "Fr", [128, 3, 16, 128], BF16)
                            Ir = sbt(sb_, "Ir", [128, 3, 2, 512], BF16)
                            dec = sbt(sb_, "dec", [128, 2, 128], F32)
                            ev = sbt(sb_, "ev", [128, 10, 128], F32)
                            for lc in range(18):
                                b = next_bank(0, 7)
                                OP("pe", lambda e, lc=lc, b=b: e.matmul(PS[:, b, 0:256], hdnT[:, lc * 128:(lc + 1) * 128],
                                                                       fwq[:].rearrange("p a c -> p (a c)"), start=True, stop=True),
                                   reads=["hdnT", "fwq"], writes=[BK(b)])
                                OP("act", lambda e, lc=lc: e.activation(dec[:, lc % 2, :], dl[:], AF.Exp, scale=negt[:, lc:lc + 1]),
                                   reads=["dl", "negt"], writes=["dec%d" % (lc % 2)])
                                for a in range(2):
                                    OP("dve", lambda e, lc=lc, b=b, a=a: e.tensor_tensor(
                                        comb[:, lc, 128 + a * 128:256 + a * 128], PS[:, b, a * 128:(a + 1) * 128], dec[:, lc % 2, :], ALU.mult),
                                       reads=["dec%d" % (lc % 2)], writes=[BK(b), "comb"])
                            for si, col in enumerate((0, CTX)):
                                OP("dve", lambda e, si=si, col=col: e.tensor_scalar(
                                    hb0[:, si, :], fwq[:, 1, :], hdnT[:, col:col + 1], None, ALU.mult),
                                   reads=["hdnT", "fwq"], writes=["hb0"])
                                b = next_bank(0, 7)
                                OP("pe", lambda e, si=si, b=b: e.matmul(PS[:, b, 0:128], ones_f32[:, :], hb0[:, si, :], start=True, stop=True),
                                   reads=["hb0", "ones_f32"], writes=[BK(b)])
                                OP("dve", lambda e, si=si, b=b: e.tensor_tensor(kb[:, si, :], fbq[:], PS[:, b, 0:128], ALU.subtract),
                                   reads=["fbq"], writes=[BK(b), "kb"])
                            for (si, lc0, nlc) in ((0, 0, 2), (1, 2, 16)):
                                for fc in range(nlc):
                                    banks = []
                                    for ab in range(2):
                                        fs = fcount[0] % 3
                                        fcount[0] += 1
                                        if si == 0:
                                            OP("sp", lambda e, fc=fc, ab=ab, fs=fs: e.dma_start(out=Fr[:, fs, 0:2, :], in_=dftFc[fc, ab]),
                                               writes=["Fr%d" % fs], dma=True)
                                        else:
                                            OP("sp", lambda e, fc=fc, ab=ab, fs=fs: e.dma_start(out=Fr[:, fs, :, :], in_=dftF[fc, ab]),
                                               writes=["Fr%d" % fs], dma=True)
                                        b = next_bank(0, 7)
                                        banks.append(b)

                                        def mm(e, fs=fs, b=b, lc0=lc0, nlc=nlc):
                                            r = None
                                            for nc_ in range(nlc):
                                                r = e.matmul(PS[:, b, 0:384], Fr[:, fs, nc_, :], comb[:, lc0 + nc_, :],
                                                             start=(nc_ == 0), stop=(nc_ == nlc - 1))
                                            return r
                                        OP("pe", mm, reads=["Fr%d" % fs, "comb"], writes=[BK(b)])
                                    bA, bB = banks
                                    yi = lc0 + fc
                                    E = lambda i: ev[:, i, :]
                                    OP("act", lambda e, bA=bA: e.copy(E(0), PS[:, bA, 0:128]), writes=[BK(bA), "ev0"])
                                    OP("act", lambda e, bB=bB: e.copy(E(1), PS[:, bB, 0:128]), writes=[BK(bB), "ev1"])
                                    OP("act", lambda e, bB=bB: e.copy(E(2), PS[:, bB, 256:384]), writes=[BK(bB), "ev2"])
                                    OP("dve", lambda e, bA=bA, si=si: e.tensor_tensor(E(3), PS[:, bA, 128:256], kb[:, si, :], ALU.add),
                                       reads=["kb"], writes=[BK(bA), "ev3"])
                                    OP("dve", lambda e, bA=bA: e.tensor_tensor(E(3), E(3), PS[:, bA, 256:384], ALU.add),
                                       writes=[BK(bA), "ev3"])
                                    OP("dve", lambda e, bB=bB: e.tensor_tensor(E(4), PS[:, bB, 128:256], E(2), ALU.subtract),
                                       reads=["ev2"], writes=[BK(bB), "ev4"])
                                    OP("dve", lambda e: e.tensor_tensor(E(5), E(0), E(3), ALU.mult), reads=["ev0", "ev3"], writes=["ev5"])
                                    OP("dve", lambda e: e.tensor_tensor(E(6), E(1), E(4), ALU.mult), reads=["ev1", "ev4"], writes=["ev6"])
                                    OP("dve", lambda e, yi=yi: e.tensor_tensor(Y[:, yi, 0, :], E(5), E(6), ALU.subtract),
                                       reads=["ev5", "ev6"], writes=["Y"])
                                    OP("pool", lambda e: e.tensor_tensor(E(7), E(0), E(4), ALU.mult), reads=["ev0", "ev4"], writes=["ev7"])
                                    OP("pool", lambda e: e.tensor_tensor(E(8), E(1), E(3), ALU.mult), reads=["ev1", "ev3"], writes=["ev8"])
                                    OP("pool", lambda e, yi=yi: e.tensor_tensor(Y[:, yi, 1, :], E(7), E(8), ALU.add),
                                       reads=["ev7", "ev8"], writes=["Y"])
                            b = next_bank(0, 7)
                            for fc in range(2):
                                is_ = icount[0] % 3
                                icount[0] += 1
                                OP("sp", lambda e, fc=fc, is_=is_: e.dma_start(out=Ir[:, is_, :, 0:256], in_=dftIc[fc]),
                                   writes=["Ir%d" % is_], dma=True)

                                def mm(e, fc=fc, is_=is_, b=b):
                                    r = None
                                    for ri in range(2):
                                        r = e.matmul(PS[:, b, 0:256], Y[:, fc, ri, :], Ir[:, is_, ri, 0:256],
                                                     start=(fc == 0 and ri == 0), stop=(fc == 1 and ri == 1))
                                    return r
                                OP("pe", mm, reads=["Y", "Ir%d" % is_], writes=[BK(b)])
                            OP("dve", lambda e, b=b: e.tensor_tensor(zrow[:, 0:CTX], x0row[:, 0:CTX], PS[:, b, 0:256], ALU.mult),
                               reads=["x0row"], writes=[BK(b), "zrow"])
                            for nt in range(4):
                                b = next_bank(0, 7)
                                for fc in range(16):
                                    is_ = icount[0] % 3
                                    icount[0] += 1
                                    OP("sp", lambda e, fc=fc, is_=is_, nt=nt: e.dma_start(out=Ir[:, is_, :, :], in_=dftI[nt, fc]),
                                       writes=["Ir%d" % is_], dma=True)

                                    def mm(e, fc=fc, is_=is_, b=b):
                                        r = None
                                        for ri in range(2):
                                            r = e.matmul(PS[:, b, :], Y[:, 2 + fc, ri, :], Ir[:, is_, ri, :],
                                                         start=(fc == 0 and ri == 0), stop=(fc == 15 and ri == 1))
                                        return r
                                    OP("pe", mm, reads=["Y", "Ir%d" % is_], writes=[BK(b)])
                                t0 = CTX + nt * 512
                                OP("dve", lambda e, b=b, t0=t0: e.tensor_tensor(zrow[:, t0:t0 + 512], x0row[:, t0:t0 + 512], PS[:, b, :], ALU.mult),
                                   reads=["x0row"], writes=[BK(b), "zrow"])
                            OP("sp", lambda e, j=j: e.dma_start(out=zs[:, j, :], in_=zrow[:]), reads=["zrow"], writes=["zs"], dma=True)
                            S.barrier()
                    for j in range(8):
                        hy_chunk(j)
                OP("sp", lambda e: e.dma_start(out=hT[:], in_=zs), reads=["zs"], writes=["hT"], dma=True)
                mixer_out(hy_w_out, hT, "hT", V_HBOUT, list(range(9)))

        def rglru(l):
            S.barrier()
            with ExitStack() as ph:
                hT = sbt(ph, "hT", [128, 8, NT], BF16)
                prenorm_all(hT, 3)
                xcb = sbt(ph, "xcb", [128, 2, NT], BF16)
                OP("act", lambda e: e.activation(softp[:], vecs[:, V_RLAM:V_RLAM + 16], AF.Exp, scale=-1.0), reads=["vecs"], writes=["softp"])
                OP("act", lambda e: e.activation(softp[:], softp[:], AF.Ln, bias=1.0), writes=["softp"])
                OP("dve", lambda e: e.tensor_scalar(softp[:], softp[:], -8.0, None, ALU.mult), writes=["softp"])
                wv = rg_w_in.rearrange("(kc p) m -> p kc m", p=128)
                def rg_head(hd):
                    S.barrier()
                    with ExitStack() as sa_:
                        upad = sbt(sa_, "upad", [128, PADW], F32)
                        acc = sbt(sa_, "acc", [128, PADW], F32)
                        win = sbt(sa_, "win", [128, 2, 8, 128], BF16)
                        OP("pool", lambda e: e.memset(upad[:], 0.0), writes=["upad"])
                        udst = udst_of(upad)
                        for jj in range(2):
                            j = 2 * hd + jj
                            oc = 8 + j
                            OP("pool", lambda e, oc=oc, jj=jj: e.dma_start(out=win[:, jj, :, :], in_=wv[:, :, oc * 128:(oc + 1) * 128]),
                               writes=["win%d" % jj], dma=True)
                            linear_rows(lambda kc, jj=jj: win[:, jj, kc, :], "win%d" % jj, hT, "hT", udst, "upad", vecs[:, V_RBIN + oc:V_RBIN + oc + 1])
                            conv_rows(upad, acc, lambda k, j=j: vecs[:, V_RCW + k * 8 + j:V_RCW + k * 8 + j + 1],
                                      vecs[:, V_RCB + j:V_RCB + j + 1], 4,
                                      [(xcb[:, jj, t:t + ln], "xcb", p, ln) for (t, p, ln) in SEGS], "pool")
                        S.barrier()
                    with ExitStack() as sb_:
                        wing = sbt(sb_, "wing", [128, 8, 128], BF16)
                        wg = sbt(sb_, "wg", [128, 2, 2, 2, 256], BF16)
                        rr = sbt(sb_, "rr", [128, NT], F32)
                        ir = sbt(sb_, "ir", [128, NT], F32)
                        ar = sbt(sb_, "ar", [128, NT], F32)
                        hsA = sbt(sb_, "hsA", [128, NT], F32)
                        gg = sbt(sb_, "gg", [128, LAT], F32)
                        zrow = sbt(sb_, "zrow", [128, LAT], BF16)
                        g2 = ar
                        for d in range(2):
                            for gi, wsrc in enumerate((rg_wa, rg_wi)):
                                OP("pool", lambda e, d=d, gi=gi, wsrc=wsrc, hd=hd: e.dma_start(
                                    out=wg[:, d, gi, :, :], in_=wsrc[d, hd].rearrange("(kc p) o -> p kc o", p=128)),
                                   writes=["wg"], dma=True)
                        for jj in range(2):
                            j = 2 * hd + jj
                            OP("pool", lambda e, j=j: e.dma_start(out=wing[:], in_=wv[:, :, j * 128:(j + 1) * 128]),
                               writes=["wing"], dma=True)
                            linear_rows(lambda kc: wing[:, kc, :], "wing", hT, "hT",
                                        lambda t0, t1: gg[:, t0 - CTX:t1 - CTX], "gg", vecs[:, V_RBIN + j:V_RBIN + j + 1], tiles=TT[1:])
                            G2 = g2[:, 0:LAT]
                            OP("pool", lambda e: e.tensor_tensor(G2, gg[:], gg[:], ALU.mult), reads=["gg"], writes=["ar"])
                            OP("pool", lambda e: e.tensor_scalar(G2, G2, 0.044715, 1.0, ALU.mult, ALU.add), writes=["ar"])
                            OP("pool", lambda e: e.tensor_tensor(G2, G2, gg[:], ALU.mult), reads=["gg"], writes=["ar"])
                            OP("act", lambda e: e.activation(G2, G2, AF.Sigmoid, scale=1.5957691216057308), writes=["ar"])
                            OP("pool", lambda e: e.tensor_tensor(gg[:], gg[:], G2, ALU.mult), reads=["ar"], writes=["gg"])
                            for d in range(2):
                                for gi, (dstrow, dres, boff) in enumerate(((rr, "rr", V_RBA), (ir, "ir", V_RBI))):
                                    for (t0, t1) in TT:
                                        T = t1 - t0
                                        b = next_bank(0, 7)

                                        def mm(e, d=d, gi=gi, jj=jj, b=b, t0=t0, t1=t1, T=T):
                                            r = None
                                            for kc in range(2):
                                                r = e.matmul(PS[:, b, 0:T], wg[:, d, gi, kc, jj * 128:(jj + 1) * 128],
                                                             xcb[:, kc, t0:t1], start=(kc == 0), stop=(kc == 1))
                                            return r
                                        OP("pe", mm, reads=["wg", "xcb"], writes=[BK(b)])
                                        OP("act", lambda e, dstrow=dstrow, d=d, j=j, boff=boff, b=b, t0=t0, t1=t1, T=T: e.activation(
                                            dstrow[:, t0:t1], PS[:, b, 0:T], AF.Sigmoid,
                                            bias=vecs[:, boff + d * 8 + j:boff + d * 8 + j + 1], scale=1.0),
                                           reads=["vecs"], writes=[BK(b), dres])
                                OP("act", lambda e, d=d, j=j: e.activation(ar[:], rr[:], AF.Exp, scale=softp[:, d * 8 + j:d * 8 + j + 1]),
                                   reads=["rr", "softp"], writes=["ar"])
                                OP("pool", lambda e: e.tensor_tensor(rr[:], ar[:], ar[:], ALU.mult), reads=["ar"], writes=["rr"])
                                OP("act", lambda e: e.activation(rr[:], rr[:], AF.Sqrt, bias=1.0, scale=-1.0), writes=["rr"])
                                OP("pool", lambda e, jj=jj: e.tensor_tensor(ir[:], ir[:], xcb[:, jj, :], ALU.mult), reads=["xcb"], writes=["ir"])
                                OP("pool", lambda e: e.tensor_tensor(ir[:], ir[:], rr[:], ALU.mult), reads=["rr"], writes=["ir"])
                                if d == 0:
                                    OP("dve", lambda e: e.tensor_tensor_scan(hsA[:, 0:CTX], ar[:, 0:CTX], ir[:, 0:CTX], 0.0, ALU.mult, ALU.add),
                                       reads=["ar", "ir"], writes=["hsAc"])
                                    OP("dve", lambda e: e.tensor_tensor_scan(hsA[:, CTX:NT], ar[:, CTX:NT], ir[:, CTX:NT],
                                                                            hsA[:, CTX - 1:CTX], ALU.mult, ALU.add),
                                       reads=["ar", "ir", "hsAc"], writes=["hsA"])
                                else:
                                    OP("dve", lambda e: e.tensor_tensor_scan(rr[:, CTX - 1::-1], ar[:, CTX - 1::-1], ir[:, CTX - 1::-1], 0.0, ALU.mult, ALU.add),
                                       reads=["ar", "ir"], writes=["rr"])
                                    OP("dve", lambda e: e.tensor_tensor_scan(rr[:, NT - 1:CTX - 1:-1], ar[:, NT - 1:CTX - 1:-1], ir[:, NT - 1:CTX - 1:-1],
                                                                            rr[:, 0:1], ALU.mult, ALU.add),
                                       reads=["ar", "ir"], writes=["rr"])
                            OP("dve", lambda e: e.tensor_tensor(hsA[:, CTX:NT], hsA[:, CTX:NT], rr[:, CTX:NT], ALU.add),
                               reads=["rr"], writes=["hsA"])
                            OP("dve", lambda e: e.tensor_tensor(zrow[:], hsA[:, CTX:NT], gg[:], ALU.mult),
                               reads=["hsA", "gg"], writes=["zrow"])
                            OP("sp", lambda e, j=j: e.dma_start(out=zs[:, j, CTX:NT], in_=zrow[:]), reads=["zrow"], writes=["zs"], dma=True)
                        S.barrier()
                for hd in range(4):
                    rg_head(hd)
                OP("sp", lambda e: e.dma_start(out=hT[:], in_=zs), reads=["zs"], writes=["hT"], dma=True)
                mixer_out(rg_w_out, hT, "hT", V_RBOUT, list(range(1, 9)))

        import os
        for i in range(4):
            if i == 0 and not os.environ.get("KDBG_NOADA"):
                ada_layer(0)
            pass
        nst = 0

        def stage():
            nonlocal nst
            nst += 1
            return nst <= stages
        if stage():
            ffn(0, 0, list(range(9)))
        if stage():
            hyena(0)
        if stage():
            ffn(0, 1, list(range(9)))
        if stage():
            ada_layer(1)
            ffn(1, 0, list(range(9)))
        if stage():
            rglru(1)
        if stage():
            ffn(1, 1, list(range(1, 9)))
        for c in range(8):
            o = OP("sp", lambda e, c=c: e.dma_start(out=out_d[:, c, :], in_=xT[:, c, CTX:NT]),
                   reads=xres_range(c, CTX, NT), dma=True)
            out_ops.append(o)
        S.finalize(final_wait_ops=out_ops)
    return nc


def _fm(v):
    v = np.asarray(v, np.float32)
    n = v.shape[-1] // 128
    v = v.reshape(v.shape[:-1] + (n, 128))
    return np.moveaxis(v, -1, 0)


def _pos_embed():
    rows = LAT // 64
    row = np.repeat(np.arange(rows, dtype=np.float32), 64)
    col = np.tile(np.arange(64, dtype=np.float32), rows)
    quarter = D // 4
    omega = (10000.0 ** (-np.arange(quarter, dtype=np.float32) / quarter)).astype(np.float32)

    def emb(p):
        ang = p[:, None] * omega[None]
        return np.concatenate([np.sin(ang), np.cos(ang)], axis=-1)
    return np.concatenate([emb(row), emb(col)], axis=-1).astype(np.float32)


def _zfeat(L):
    t = np.linspace(0.0, 1.0, L, dtype=np.float32)[:, None]
    w = (2.0 * math.pi * np.arange(L, dtype=np.float32)[:, None] / L).astype(np.float32)
    f = np.linspace(1e-4, 15, 16, dtype=np.float32)[None]
    phase = f * w
    return np.concatenate([t, np.cos(phase), -np.sin(phase)], axis=-1).astype(np.float32)


def _dft(L):
    N = 2 * L
    n = np.arange(L, dtype=np.float64)[:, None]
    f = np.arange(L, dtype=np.float64)[None, :]
    th = 2 * np.pi * (f + 0.5) * n / N
    return np.cos(th), -np.sin(th)


_CONST = {}


def _constants():
    if _CONST:
        return _CONST
    bf = ml_dtypes.bfloat16
    pe = _pos_embed()
    _CONST["pos"] = np.ascontiguousarray(pe.T.reshape(8, 128, LAT).transpose(1, 0, 2))
    zT = np.concatenate([_zfeat(CTX), _zfeat(LAT)], axis=0).T
    _CONST["zT"] = np.ascontiguousarray(zT, np.float32)
    min_decay = math.log(1e-2) / 1.5
    max_decay = math.log(1e-2) / 0.3
    deltas = np.abs(np.linspace(min_decay, max_decay, D, dtype=np.float32))
    _CONST["delta_b"] = np.ascontiguousarray(np.broadcast_to(deltas[None, :], (128, D)), np.float32)
    negt = np.zeros((128, 18), np.float32)
    tc = np.linspace(0.0, 1.0, CTX, dtype=np.float32)
    tl = np.linspace(0.0, 1.0, LAT, dtype=np.float32)
    negt[:, 0:2] = -tc.reshape(2, 128).T
    negt[:, 2:18] = -tl.reshape(16, 128).T
    _CONST["negt"] = negt
    A, B = _dft(LAT)
    M = np.stack([A, B])
    F = M.reshape(2, 16, 128, 16, 128).transpose(3, 0, 2, 1, 4)
    _CONST["dftF"] = np.ascontiguousarray(F).astype(bf)
    I = (M * (2.0 / (2 * LAT))).reshape(2, 4, 512, 16, 128).transpose(1, 3, 4, 0, 2)
    _CONST["dftI"] = np.ascontiguousarray(I).astype(bf)
    A, B = _dft(CTX)
    M = np.stack([A, B])
    F = M.reshape(2, 2, 128, 2, 128).transpose(3, 0, 2, 1, 4)
    _CONST["dftFc"] = np.ascontiguousarray(F).astype(bf)
    I = (M * (2.0 / (2 * CTX))).reshape(2, 256, 2, 128).transpose(2, 3, 0, 1)
    _CONST["dftIc"] = np.ascontiguousarray(I).astype(bf)
    return _CONST


_NC_CACHE = {}


def kernel(**inp):
    f32 = lambda a: np.ascontiguousarray(np.asarray(a, np.float32))
    cst = _constants()
    B = inp["x"].shape[0]
    vec = np.zeros((128, NV), np.float32)
    vec[:, V_ADAB:V_ADAB + 144] = _fm(inp["ada_b"]).reshape(128, 144)
    vec[:, V_NG:V_NG + 96] = _fm(inp["norm_g"]).reshape(128, 96)
    vec[:, V_HBIN:V_HBIN + 24] = _fm(inp["hy_b_in"][0])
    vec[:, V_HCW:V_HCW + 72] = _fm(inp["hy_conv_w"][0]).reshape(128, 72)
    vec[:, V_HCB:V_HCB + 24] = _fm(inp["hy_conv_b"][0])
    vec[:, V_HBOUT:V_HBOUT + 8] = _fm(inp["hy_b_out"][0])
    vec[:, V_RBIN:V_RBIN + 16] = _fm(inp["rg_b_in"][0])
    vec[:, V_RCW:V_RCW + 32] = _fm(inp["rg_conv_w"][0]).reshape(128, 32)
    vec[:, V_RCB:V_RCB + 8] = _fm(inp["rg_conv_b"][0])
    vec[:, V_RBA:V_RBA + 16] = _fm(inp["rg_ba"][0]).reshape(128, 16)
    vec[:, V_RBI:V_RBI + 16] = _fm(inp["rg_bi"][0]).reshape(128, 16)
    vec[:, V_RLAM:V_RLAM + 16] = _fm(inp["rg_lam"][0]).reshape(128, 16)
    vec[:, V_RBOUT:V_RBOUT + 8] = _fm(inp["rg_b_out"][0])
    fvec = np.stack([inp["hy_fb0"][0], inp["hy_fb1"][0], inp["hy_fb2"][0],
                     inp["hy_freq"][0, 0], inp["hy_freq"][0, 1], inp["hy_freq"][0, 2]], axis=1)
    shared = {
        "pos": cst["pos"], "ada_w": f32(inp["ada_w"]), "vecs": vec,
        "ffn_w1": f32(inp["ffn_w1"]), "ffn_w3": f32(inp["ffn_w3"]), "ffn_w2": f32(inp["ffn_w2"]),
        "hy_w_in": f32(inp["hy_w_in"][0]), "hy_w_out": f32(inp["hy_w_out"][0]),
        "rg_w_in": f32(inp["rg_w_in"][0]), "rg_w_out": f32(inp["rg_w_out"][0]),
        "rg_wa": f32(inp["rg_wa"][0]), "rg_wi": f32(inp["rg_wi"][0]),
        "zT": cst["zT"], "fw0": f32(inp["hy_fw0"][0]), "fw1": f32(inp["hy_fw1"][0]), "fw2": f32(inp["hy_fw2"][0]),
        "fvec": f32(fvec), "fwout": f32(inp["hy_fwout"][0]),
        "delta_b": cst["delta_b"], "negt": cst["negt"],
        "fbias_b": f32(np.broadcast_to(np.asarray(inp["hy_filt_bias"][0], np.float32)[None, :], (128, D))),
        "dftF": cst["dftF"], "dftFc": cst["dftFc"], "dftI": cst["dftI"], "dftIc": cst["dftIc"],
    }
    cctx = np.asarray(inp["c_ctx"], np.float32)
    in_maps = []
    for b in range(B):
        m = dict(shared)
        m["xin"] = f32(np.asarray(inp["x"][b], np.float32).T.reshape(8, 128, LAT).transpose(1, 0, 2))
        m["cin"] = f32(np.asarray(inp["ctx"][b], np.float32).T.reshape(8, 128, CTX).transpose(1, 0, 2))
        cv = np.stack([np.asarray(inp["c"][b], np.float32), cctx], axis=-1)
        m["cvec"] = f32(cv.reshape(8, 128, 2).transpose(1, 0, 2))
        in_maps.append(m)
    if "nc" not in _NC_CACHE:
        _NC_CACHE["nc"] = build_program()
    res = run_bass_kernel_spmd(_NC_CACHE["nc"], in_maps, core_ids=list(range(B)))
    outs = []
    for b in range(B):
        o = np.asarray(res.results[b]["out"], np.float32)
        outs.append(o.transpose(1, 0, 2).reshape(D, LAT).T)
    return np.ascontiguousarray(np.stack(outs, axis=0), np.float32)
```

```python
import math
from contextlib import ExitStack

import numpy as np
import ml_dtypes
import concourse.bass as bass
import concourse.mybir as mybir
from concourse.bass_utils import run_bass_kernel_spmd

F32 = mybir.dt.float32
BF16 = mybir.dt.bfloat16
AF = mybir.ActivationFunctionType
ALU = mybir.AluOpType

D = 1024
DFF = 2816
NFC = 22
LAT = 2048
CTX = 256
NT = LAT + CTX
EPS = 1e-6
STAGES = 99

ENGS = ("pe", "act", "dve", "pool", "sp")


class Op:
    __slots__ = ("eng", "emit", "deps", "dma", "sem", "val", "signal")

    def __init__(self, eng, emit, dma):
        self.eng = eng
        self.emit = emit
        self.deps = []
        self.dma = dma
        self.sem = None
        self.val = None
        self.signal = False


class Sched:
    def __init__(self, nc, n_dma_sems=16):
        self.nc = nc
        self.ops = []
        self.last_w = {}
        self.readers = {}
        self.n_dma_sems = n_dma_sems
        self.bar_pos = 0

    def op(self, eng, emit, reads=(), writes=(), dma=False):
        o = Op(eng, emit, dma)
        deps = {}
        for r in reads:
            w = self.last_w.get(r)
            if w is not None:
                deps[id(w)] = w
        for r in writes:
            w = self.last_w.get(r)
            if w is not None:
                deps[id(w)] = w
            for rd in self.readers.get(r, ()):
                deps[id(rd)] = rd
        for d in deps.values():
            if d.eng == "pe" and eng == "pe" and not d.dma and not dma:
                continue
            o.deps.append(d)
            d.signal = True
        for r in reads:
            self.readers.setdefault(r, []).append(o)
        for r in writes:
            self.last_w[r] = o
            self.readers[r] = []
        self.ops.append(o)
        return o

    def barrier(self, skip=()):
        skip = set(id(o) for o in skip)
        lasts = {}
        for o in self.ops:
            if o.emit is not None and not o.dma:
                lasts[o.eng] = o
        dmas = [o for o in self.ops[self.bar_pos:] if o.dma and id(o) not in skip]
        self.bar_pos = len(self.ops)
        for e in ENGS:
            b = Op(e, None, False)
            b.deps = list(lasts.values()) + dmas
            for d in b.deps:
                d.signal = True
            self.ops.append(b)

    def finalize(self, final_wait_ops=()):
        nc = self.nc
        for o in final_wait_ops:
            o.signal = True
        with ExitStack() as es:
            esem = {e: es.enter_context(nc.semaphore("s_" + e)) for e in ENGS}
            dsem = {e: [es.enter_context(nc.semaphore("d_%s_%d" % (e, i)))
                        for i in range(self.n_dma_sems)] for e in ("sp", "pool", "act")}
            ecount = {e: 0 for e in ENGS}
            dcount = {e: [0] * self.n_dma_sems for e in dsem}
            drr = {e: 0 for e in dsem}
            prev_on_sem = {}
            for o in self.ops:
                if o.dma:
                    i = drr[o.eng]
                    drr[o.eng] = (i + 1) % self.n_dma_sems
                    dcount[o.eng][i] += 16
                    o.sem = dsem[o.eng][i]
                    o.val = dcount[o.eng][i]
                    p = prev_on_sem.get((o.eng, i))
                    if p is not None:
                        o.deps.append(p)
                    prev_on_sem[(o.eng, i)] = o
                elif o.signal:
                    ecount[o.eng] += 1
                    o.sem = esem[o.eng]
                    o.val = ecount[o.eng]
            per_eng = {e: [o for o in self.ops if o.eng == e] for e in ENGS}
            final_waits = [(o.sem, o.val) for o in final_wait_ops]
            with nc.Block() as block:
                def run(eng_name, eng):
                    seen = {}
                    for o in per_eng[eng_name]:
                        need = {}
                        for d in o.deps:
                            k = id(d.sem)
                            if seen.get(k, 0) >= d.val:
                                continue
                            if k not in need or need[k][1] < d.val:
                                need[k] = (d.sem, d.val)
                        for k, (s, v) in need.items():
                            eng.wait_ge(s, v)
                            seen[k] = v
                        if o.emit is None:
                            continue
                        inst = o.emit(eng)
                        if o.dma:
                            inst.then_inc(o.sem, 16)
                        elif o.signal:
                            inst.then_inc(o.sem, 1)
                    if eng_name == "sp":
                        for (s, v) in final_waits:
                            eng.wait_ge(s, v)

                @block.tensor
                def _(e):
                    run("pe", e)

                @block.scalar
                def _(e):
                    run("act", e)

                @block.vector
                def _(e):
                    run("dve", e)

                @block.gpsimd
                def _(e):
                    run("pool", e)

                @block.sync
                def _(e):
                    run("sp", e)


V_ADAB = 0
V_NG = 144
V_HBIN = 240
V_HCW = 264
V_HCB = 336
V_HBOUT = 360
V_RBIN = 368
V_RCW = 384
V_RCB = 416
V_RBA = 424
V_RBI = 440
V_RLAM = 456
V_RBOUT = 472
NV = 480

PADW = 2310
TT = [(0, 256)] + [(256 + 512 * i, 256 + 512 * (i + 1)) for i in range(4)]
FT = [(256 * i, 256 * (i + 1)) for i in range(9)]


def poff(t):
    return t + 2 if t < CTX else t + 4


def build_program(stages=STAGES):
    nc = bass.Bass("TRN2", target_bir_lowering=False)

    def din(name, shape, dt=F32):
        return nc.dram_tensor(name, list(shape), dt, kind="ExternalInput").ap()

    xin = din("xin", [128, 8, LAT])
    cin = din("cin", [128, 8, CTX])
    pos = din("pos", [128, 8, LAT])
    cvec = din("cvec", [128, 8, 2])
    ada_w = din("ada_w", [2, D, 9 * D])
    vecs_d = din("vecs", [128, NV])
    w1_d = din("ffn_w1", [2, 2, D, DFF])
    w3_d = din("ffn_w3", [2, 2, D, DFF])
    w2_d = din("ffn_w2", [2, 2, DFF, D])
    hy_w_in = din("hy_w_in", [D, 3 * D])
    hy_w_out = din("hy_w_out", [D, D])
    rg_w_in = din("rg_w_in", [D, 2 * D])
    rg_w_out = din("rg_w_out", [D, D])
    rg_wa = din("rg_wa", [2, 4, 256, 256])
    rg_wi = din("rg_wi", [2, 4, 256, 256])
    zT_d = din("zT", [33, NT])
    fw0_d = din("fw0", [33, 64])
    fw1_d = din("fw1", [64, 64])
    fw2_d = din("fw2", [64, 64])
    fvec_d = din("fvec", [64, 6])
    fwout_d = din("fwout", [64, 2 * D])
    delta_d = din("delta_b", [128, D])
    negt_d = din("negt", [128, 18])
    fbias_d = din("fbias_b", [128, D])
    dftF = din("dftF", [16, 2, 128, 16, 128], BF16)
    dftFc = din("dftFc", [2, 2, 128, 2, 128], BF16)
    dftI = din("dftI", [4, 16, 128, 2, 512], BF16)
    dftIc = din("dftIc", [2, 128, 2, 256], BF16)
    out_d = nc.dram_tensor("out", [128, 8, LAT], F32, kind="ExternalOutput").ap()
    zs = nc.dram_tensor("zs", [128, 8, NT], BF16, kind="ExternalOutput").ap()

    S = Sched(nc)
    OP = S.op
    out_ops = []

    with ExitStack() as top:
        uniq = [0]

        def sbt(es, name, shape, dt):
            uniq[0] += 1
            return es.enter_context(nc.sbuf_tensor("sb%d_%s" % (uniq[0], name), list(shape), dt))

        PS = top.enter_context(nc.psum_tensor("ps", [128, 8, 512], F32))
        xT = sbt(top, "xT", [128, 8, NT], F32)
        vecs = sbt(top, "vecs", [128, NV], F32)
        tab = sbt(top, "tab", [128, 2, 2, 9, 8], F32)
        ones = sbt(top, "ones", [128, 128], BF16)
        ident = sbt(top, "ident", [128, 128], BF16)
        identf = sbt(top, "identf", [128, 128], F32)
        softp = sbt(top, "softp", [128, 16], F32)

        bank_rr = [0]

        def next_bank(lo=0, hi=8):
            b = lo + bank_rr[0] % (hi - lo)
            bank_rr[0] += 1
            return b

        def BK(b):
            return "bank%d" % b

        OP("sp", lambda e: e.dma_start(out=vecs[:], in_=vecs_d), writes=["vecs"], dma=True)
        OP("pool", lambda e: e.memset(ones[:], 1.0), writes=["ones"])
        OP("pool", lambda e: e.memset(identf[:], 0.0), writes=["identf"])
        OP("pool", lambda e: e.affine_select(out=identf[:], in_=identf[:], pattern=[[-1, 128]],
                                             compare_op=ALU.not_equal, fill=1.0, base=0,
                                             channel_multiplier=1), writes=["identf"])
        OP("dve", lambda e: e.tensor_copy(ident[:], identf[:]), reads=["identf"], writes=["ident"])

        def xres(c, t0):
            return "x%d_%d" % (c, t0 // 256)

        def xres_range(c, t0, t1):
            return [xres(c, t) for t in range(t0, t1, 256)]

        with ExitStack() as ph:
            ptile = sbt(ph, "ptile", [128, 2, 8, 256], F32)
            OP("sp", lambda e: e.dma_start(out=xT[:, :, 0:CTX], in_=cin),
               writes=[xres(c, 0) for c in range(8)], dma=True)
            for c in range(8):
                OP("sp", lambda e, c=c: e.dma_start(out=xT[:, c, CTX:NT], in_=xin[:, c, :]),
                   writes=xres_range(c, CTX, NT), dma=True)
            for i in range(8):
                t0 = i * 256
                OP("sp", lambda e, i=i, t0=t0: e.dma_start(out=ptile[:, i % 2, :, :], in_=pos[:, :, t0:t0 + 256]),
                   writes=["ptile%d" % (i % 2)], dma=True)
                eng = "dve" if i % 2 == 0 else "pool"
                OP(eng, lambda e, i=i, t0=t0: e.tensor_tensor(
                    xT[:, :, CTX + t0:CTX + t0 + 256], xT[:, :, CTX + t0:CTX + t0 + 256],
                    ptile[:, i % 2, :, :], ALU.add),
                   reads=["ptile%d" % (i % 2)], writes=[xres(c, CTX + t0) for c in range(8)])

        ADA_BANK = 6

        def ada_thunks(l, es):
            aw = sbt(es, "aw", [128, 2, 8, 512], F32)
            sc = sbt(es, "sc", [128, 8, 2], F32)
            modv = sbt(es, "modv", [128, 2, 72], F32)
            OP("sp", lambda e: e.dma_start(out=sc[:], in_=cvec), writes=["sc"], dma=True)
            OP("act", lambda e: e.activation(sc[:], sc[:], AF.Silu), reads=["sc"], writes=["sc"])
            awv = ada_w[l].rearrange("(kc p) m -> p kc m", p=128)
            bank = ADA_BANK
            th = []

            def piece(i):
                OP("sp", lambda e: e.dma_start(out=aw[:, i % 2, :, :], in_=awv[:, :, i * 512:(i + 1) * 512]),
                   writes=["aw%d" % (i % 2)], dma=True)

                def mm(e):
                    r = None
                    for m in range(4):
                        mc = i * 4 + m
                        for kc in range(8):
                            r = e.matmul(PS[:, bank, 2 * mc:2 * mc + 2], aw[:, i % 2, kc, m * 128:(m + 1) * 128],
                                         sc[:, kc, :], start=(kc == 0), stop=(kc == 7), skip_group_check=True)
                    return r
                OP("pe", mm, reads=["aw%d" % (i % 2), "sc"], writes=["adabank"])
            for i in range(18):
                th.append(lambda i=i: piece(i))

            def fin():
                for s_ in range(2):
                    OP("dve", lambda e, s_=s_: e.tensor_tensor(modv[:, s_, :], PS[:, bank, s_:144:2],
                                                              vecs[:, V_ADAB + l * 72:V_ADAB + (l + 1) * 72], ALU.add),
                       reads=["vecs"], writes=["adabank", "modv"])

                def g(i):
                    return vecs[:, V_NG + (l * 6 + i) * 8:V_NG + (l * 6 + i) * 8 + 8]

                def mo(s_, i):
                    return modv[:, s_, i * 8:(i + 1) * 8]
                tr = "tab%d" % l
                for s_ in range(2):
                    for sub in range(3):
                        gpre, gpost = g(2 * sub), g(2 * sub + 1)
                        mac = 0.5 if sub != 1 else 1.0
                        OP("dve", lambda e, s_=s_, sub=sub, gpre=gpre: e.scalar_tensor_tensor(
                            tab[:, l, s_, 3 * sub, :], mo(s_, 3 * sub + 1), 1.0, gpre, ALU.add, ALU.mult),
                           reads=["modv", "vecs"], writes=[tr])
                        OP("dve", lambda e, s_=s_, sub=sub: e.tensor_copy(tab[:, l, s_, 3 * sub + 1, :], mo(s_, 3 * sub)),
                           reads=["modv"], writes=[tr])
                        OP("dve", lambda e, s_=s_, sub=sub, gpost=gpost, mac=mac: e.scalar_tensor_tensor(
                            tab[:, l, s_, 3 * sub + 2, :], mo(s_, 3 * sub + 2), mac, gpost, ALU.mult, ALU.mult),
                           reads=["modv", "vecs"], writes=[tr])
            th.append(fin)
            return th

        def ada_layer(l, skip=()):
            S.barrier(skip)
            with ExitStack() as ph:
                for t in ada_thunks(l, ph):
                    t()
                S.barrier(skip)

        def TAB(l, s, q, c):
            return tab[:, l, s, q, c:c + 1]

        def prenorm_thunks(t0, t1, l, s, qA, sq, sd, rstd, tmp, hdst, hres):
            T = t1 - t0
            bank = 7
            tr = "tab%d" % l
            A = [TAB(l, s, qA, c) for c in range(8)]
            Bv = [TAB(l, s, qA + 1, c) for c in range(8)]

            def p1():
                OP("act", lambda e: e.activation(sq[:, :, 0:T], xT[:, :, t0:t1], AF.Square),
                   reads=[xres(c, t0) for c in range(8)], writes=["sq%d" % c for c in range(8)])

            def p2():
                def mm(e):
                    r = None
                    for c in range(8):
                        r = e.matmul(PS[:, bank, 0:T], ones[:], sq[:, c, 0:T], start=(c == 0), stop=(c == 7))
                    return r
                OP("pe", mm, reads=["sq%d" % c for c in range(8)] + ["ones"], writes=[BK(bank)])
                OP("act", lambda e: e.activation(sd[:, 0:T], PS[:, bank, 0:T], AF.Sqrt, bias=EPS, scale=1.0 / D),
                   writes=[BK(bank), "sd"])
                OP("dve", lambda e: e.reciprocal(rstd[:, 0:T], sd[:, 0:T]), reads=["sd"], writes=["rstd"])

            def p3(cs):
                for c in cs:
                    OP("dve", lambda e, c=c: e.scalar_tensor_tensor(tmp[:, c % 2, 0:T], xT[:, c, t0:t1], A[c],
                                                                   rstd[:, 0:T], ALU.mult, ALU.mult),
                       reads=[xres(c, t0), "rstd", tr], writes=["tmp%d" % (c % 2)])
                    OP("act", lambda e, c=c: e.activation(hdst(c), tmp[:, c % 2, 0:T], AF.Identity,
                                                         bias=Bv[c], scale=1.0),
                       reads=["tmp%d" % (c % 2), tr], writes=[hres])
            return [p1, p2, lambda: p3(range(0, 4)), lambda: p3(range(4, 8))]

        def prenorm_tile(*a):
            for t in prenorm_thunks(*a):
                t()

        def postnorm_thunks(t0, t1, l, s, qC, ysrc, yres, sq, sd, rstd, tmp):
            T = t1 - t0
            bank = 7
            th = []
            tr = "tab%d" % l
            Cv = [TAB(l, s, qC, dc) for dc in range(8)]

            def p1(dcs):
                for dc in dcs:
                    OP("act", lambda e, dc=dc: e.activation(sq[:, dc, 0:T], ysrc(dc), AF.Square),
                       writes=[yres(dc), "sq%d" % dc])
            th.append(lambda: p1(range(0, 4)))
            th.append(lambda: p1(range(4, 8)))

            def p2():
                def mm(e):
                    r = None
                    for c in range(8):
                        r = e.matmul(PS[:, bank, 0:T], ones[:], sq[:, c, 0:T], start=(c == 0), stop=(c == 7))
                    return r
                OP("pe", mm, reads=["sq%d" % dc for dc in range(8)] + ["ones"], writes=[BK(bank)])
                OP("act", lambda e: e.activation(sd[:, 0:T], PS[:, bank, 0:T], AF.Sqrt, bias=EPS, scale=1.0 / D),
                   writes=[BK(bank), "sd"])
                OP("dve", lambda e: e.reciprocal(rstd[:, 0:T], sd[:, 0:T]), reads=["sd"], writes=["rstd"])
            th.append(p2)

            def p3(dcs):
                for dc in dcs:
                    OP("dve", lambda e, dc=dc: e.scalar_tensor_tensor(tmp[:, dc % 2, 0:T], ysrc(dc), Cv[dc],
                                                                     rstd[:, 0:T], ALU.mult, ALU.mult),
                       reads=["rstd", tr], writes=[yres(dc), "tmp%d" % (dc % 2)])
                    OP("pool", lambda e, dc=dc: e.tensor_tensor(xT[:, dc, t0:t1], xT[:, dc, t0:t1],
                                                               tmp[:, dc % 2, 0:T], ALU.add),
                       reads=["tmp%d" % (dc % 2)], writes=[xres(dc, t0)])
            for d0 in range(0, 8, 2):
                th.append(lambda d0=d0: p3(range(d0, d0 + 2)))
            return th

        def postnorm_tile(*a):
            for t in postnorm_thunks(*a):
                t()

        def ffn(l, k, tiles, pre=None):
            fi = l * 2 + k
            q0 = 0 if k == 0 else 6
            S.barrier()
            with ExitStack() as ph:
                W1 = sbt(ph, "W1", [128, 8, DFF], BF16)
                W3 = sbt(ph, "W3", [128, 8, DFF], BF16)
                w1v = w1_d[l, k].rearrange("(kc p) f -> p kc f", p=128)
                w3v = w3_d[l, k].rearrange("(kc p) f -> p kc f", p=128)
                wloads = []
                for j in range(11):
                    wloads.append(OP("pool", lambda e, j=j: e.dma_start(out=W1[:, :, j * 256:(j + 1) * 256], in_=w1v[:, :, j * 256:(j + 1) * 256]),
                                     writes=["W1_%d" % j], dma=True))
                    wloads.append(OP("pool", lambda e, j=j: e.dma_start(out=W3[:, :, j * 256:(j + 1) * 256], in_=w3v[:, :, j * 256:(j + 1) * 256]),
                                     writes=["W3_%d" % j], dma=True))
                if pre is not None:
                    pre(wloads)
                w2r = sbt(ph, "w2r", [128, 3, 2, D], BF16)
                hbuf = sbt(ph, "hbuf", [128, 2, 8, 256], BF16)
                hid = sbt(ph, "hid", [128, 4, 256], BF16)
                sa = sbt(ph, "sa", [128, 2, 256], F32)
                sq = sbt(ph, "sq", [128, 8, 256], BF16)
                sd = sbt(ph, "sd", [128, 256], F32)
                rstd = sbt(ph, "rstd", [128, 256], F32)
                tmp = sbt(ph, "tmp", [128, 2, 256], F32)
                ysb = sbt(ph, "ysb", [128, 8, 256], F32)
                ysbf = ysb[:].rearrange("p c t -> p (c t)")
                pending = {}
                w2v = w2_d[l, k].rearrange("(fc p) d -> p fc d", p=128)
                npieces = len(tiles) * 11

                def w2_load(gp):
                    if gp >= npieces:
                        return
                    j = gp % 11
                    OP("pool", lambda e, gp=gp, j=j: e.dma_start(out=w2r[:, gp % 3, :, :], in_=w2v[:, 2 * j:2 * j + 2, :]),
                       writes=["w2r%d" % (gp % 3)], dma=True)
                w2_load(0)
                w2_load(1)

                def pre_thunks(ti):
                    t0, t1 = FT[tiles[ti]]
                    s = 1 if tiles[ti] == 0 else 0
                    return prenorm_thunks(t0, t1, l, s, q0, sq, sd, rstd, tmp,
                                          lambda c, ti=ti: hbuf[:, ti % 2, c, :], "hbuf%d" % (ti % 2))
                for t in pre_thunks(0):
                    t()
                ring = [0]
                hcount = [0]
                for ti in range(len(tiles)):
                    t0, t1 = FT[tiles[ti]]
                    s = 1 if tiles[ti] == 0 else 0
                    if ti + 1 < len(tiles):
                        for fpos, t in zip((13, 16, 18, 19), pre_thunks(ti + 1)):
                            pending.setdefault(fpos, []).append(t)
                    hb = ti % 2
                    slots = {}

                    def hid_stage(f, hb=hb):
                        ba = 4 + ring[0] % 3
                        bb = 4 + (ring[0] + 1) % 3
                        ring[0] += 2
                        hs = hcount[0] % 4
                        hcount[0] += 1
                        slots[f] = hs
                        j = f // 2

                        def mma(e):
                            r = None
                            for kc in range(8):
                                r = e.matmul(PS[:, ba, 0:256], W1[:, kc, f * 128:(f + 1) * 128], hbuf[:, hb, kc, :],
                                             start=(kc == 0), stop=(kc == 7))
                            return r

                        def mmb(e):
                            r = None
                            for kc in range(8):
                                r = e.matmul(PS[:, bb, 0:256], W3[:, kc, f * 128:(f + 1) * 128], hbuf[:, hb, kc, :],
                                             start=(kc == 0), stop=(kc == 7))
                            return r
                        OP("pe", mma, reads=["W1_%d" % j, "hbuf%d" % hb], writes=[BK(ba)])
                        OP("pe", mmb, reads=["W3_%d" % j, "hbuf%d" % hb], writes=[BK(bb)])
                        OP("act", lambda e: e.activation(sa[:, f % 2, :], PS[:, ba, 0:256], AF.Silu),
                           writes=[BK(ba), "sa%d" % (f % 2)])
                        OP("dve", lambda e: e.tensor_tensor(hid[:, hs, :], sa[:, f % 2, :], PS[:, bb, 0:256], ALU.mult),
                           reads=["sa%d" % (f % 2)], writes=[BK(bb), "hid%d" % hs])

                    def w2_stage(f):
                        gp = ti * 11 + f // 2
                        if f % 2 == 0:
                            w2_load(gp + 2)
                        hs = slots[f]

                        def mm(e):
                            r = None
                            for dc in range(8):
                                r = e.matmul(PS[:, dc // 2, (dc % 2) * 256:(dc % 2) * 256 + 256],
                                             w2r[:, gp % 3, f % 2, dc * 128:(dc + 1) * 128], hid[:, hs, :],
                                             start=(f == 0 and dc % 2 == 0), stop=(f == NFC - 1),
                                             skip_group_check=True)
                            return r
                        OP("pe", mm, reads=["w2r%d" % (gp % 3), "hid%d" % hs], writes=[BK(b) for b in range(4)])
                    DEPTH = 2
                    for f in range(NFC + DEPTH):
                        if f < NFC:
                            hid_stage(f)
                        if f >= DEPTH:
                            w2_stage(f - DEPTH)
                        for t in pending.pop(f, []):
                            t()
                    for fpos in sorted(pending):
                        for t in pending[fpos]:
                            t()
                    pending.clear()
                    for b in range(4):
                        if b % 2 == 0:
                            OP("act", lambda e, b=b: e.copy(ysbf[:, b * 512:(b + 1) * 512], PS[:, b, :]),
                               writes=[BK(b), "ysb%d" % (2 * b), "ysb%d" % (2 * b + 1)])
                        else:
                            OP("dve", lambda e, b=b: e.tensor_copy(ysbf[:, b * 512:(b + 1) * 512], PS[:, b, :]),
                               writes=[BK(b), "ysb%d" % (2 * b), "ysb%d" % (2 * b + 1)])
                    post = postnorm_thunks(t0, t1, l, s, q0 + 2, lambda dc: ysb[:, dc, :],
                                           lambda dc: "ysb%d" % dc, sq, sd, rstd, tmp)
                    for fpos, t in zip((2, 3, 7, 9, 10, 11, 12), post):
                        pending.setdefault(fpos, []).append(t)
                for fpos in sorted(pending):
                    for t in pending[fpos]:
                        t()

        def linear_rows(wsrc, wres, src, srcres, dst, dstres, bias_ap, tiles=TT, func=AF.Identity):
            for (t0, t1) in tiles:
                T = t1 - t0
                b = next_bank(0, 7)

                def mm(e, t0=t0, t1=t1, b=b, T=T):
                    r = None
                    for kc in range(8):
                        r = e.matmul(PS[:, b, 0:T], wsrc(kc), src[:, kc, t0:t1], start=(kc == 0), stop=(kc == 7))
                    return r
                OP("pe", mm, reads=[wres, srcres], writes=[BK(b)])
                OP("act", lambda e, t0=t0, t1=t1, b=b, T=T: e.activation(dst(t0, t1), PS[:, b, 0:T], func,
                                                                        bias=bias_ap, scale=1.0),
                   reads=["vecs"], writes=[BK(b), dstres])

        def conv_rows(upad, acc, cw, cb, K, dst_segs, eng, ures="upad"):
            eng = "dve"
            n = PADW - 4
            OP(eng, lambda e: e.tensor_scalar(acc[:, 2:2 + n], upad[:, 1:1 + n], cw(0), cb, ALU.mult, ALU.add),
               reads=[ures, "vecs"], writes=["acc"])
            for k in range(1, K - 1):
                OP(eng, lambda e, k=k: e.scalar_tensor_tensor(acc[:, 2:2 + n], upad[:, 1 + k:1 + k + n], cw(k),
                                                             acc[:, 2:2 + n], ALU.mult, ALU.add),
                   reads=[ures, "vecs", "acc"], writes=["acc"])
            k = K - 1
            for (dst, dres, p0, ln) in dst_segs:
                OP(eng, lambda e, dst=dst, p0=p0, ln=ln: e.scalar_tensor_tensor(
                    dst, upad[:, p0 + k - 1:p0 + k - 1 + ln], cw(k), acc[:, p0:p0 + ln], ALU.mult, ALU.add),
                   reads=[ures, "vecs", "acc"], writes=[dres])

        ones_f32 = sbt(top, "ones_f32", [64, 128], F32)
        OP("pool", lambda e: e.memset(ones_f32[:], 1.0), writes=["ones_f32"])
        SEGS = [(0, 2, CTX), (CTX, 4 + CTX, LAT)]

        def udst_of(upad):
            return lambda t0, t1: upad[:, poff(t0):poff(t0) + (t1 - t0)]

        def norm_bufs(es):
            return (sbt(es, "sq", [128, 8, 256], BF16), sbt(es, "sd", [128, 256], F32),
                    sbt(es, "rstd", [128, 256], F32), sbt(es, "tmp", [128, 2, 256], F32))

        def prenorm_all(hT, q, l):
            S.barrier()
            with ExitStack() as es:
                sq, sd, rstd, tmp = norm_bufs(es)
                for ti in range(9):
                    t0, t1 = FT[ti]
                    s = 1 if ti == 0 else 0
                    prenorm_tile(t0, t1, l, s, q, sq, sd, rstd, tmp, lambda c, t0=t0, t1=t1: hT[:, c, t0:t1], "hT")
                S.barrier()

        def mixer_out(w_out_d, zT, zres, bout_off, tiles, l, ada_next=None):
            S.barrier()
            with ExitStack() as es:
                sq, sd, rstd, tmp = norm_bufs(es)
                extra = ada_thunks(ada_next, es) if ada_next is not None else []
                wo = sbt(es, "wo", [128, 8, D], BF16)
                ysb = sbt(es, "ysb", [128, 8, 256], F32)
                wov = w_out_d.rearrange("(kc p) m -> p kc m", p=128)
                for h in range(2):
                    OP("pool", lambda e, h=h: e.dma_start(out=wo[:, :, h * 512:(h + 1) * 512], in_=wov[:, :, h * 512:(h + 1) * 512]),
                       writes=["wo%d" % h], dma=True)
                for ti in tiles:
                    t0, t1 = FT[ti]
                    s = 1 if ti == 0 else 0
                    for oc in range(8):
                        b = next_bank(0, 6)

                        def mm(e, oc=oc, b=b, t0=t0, t1=t1):
                            r = None
                            for kc in range(8):
                                r = e.matmul(PS[:, b, 0:256], wo[:, kc, oc * 128:(oc + 1) * 128], zT[:, kc, t0:t1],
                                             start=(kc == 0), stop=(kc == 7))
                            return r
                        OP("pe", mm, reads=["wo%d" % (oc // 4), zres], writes=[BK(b)])
                        OP("act", lambda e, oc=oc, b=b: e.activation(ysb[:, oc, :], PS[:, b, 0:256], AF.Identity,
                                                                    bias=vecs[:, bout_off + oc:bout_off + oc + 1], scale=1.0),
                           reads=["vecs"], writes=[BK(b), "ysb%d" % oc])
                    postnorm_tile(t0, t1, l, s, 5, lambda dc: ysb[:, dc, :], lambda dc: "ysb%d" % dc, sq, sd, rstd, tmp)
                    for _ in range(3):
                        if extra:
                            extra.pop(0)()
                while extra:
                    extra.pop(0)()
                S.barrier()

        def hyena(l):
            S.barrier()
            with ExitStack() as ph:
                hT = sbt(ph, "hT", [128, 8, NT], BF16)
                hdnT = sbt(ph, "hdnT", [64, NT], F32)
                prenorm_all(hT, 3, l)
                with ExitStack() as fp:
                    zT = sbt(fp, "zT", [33, NT], F32)
                    fw0 = sbt(fp, "fw0", [33, 64], F32)
                    fw1 = sbt(fp, "fw1", [64, 64], F32)
                    fw2 = sbt(fp, "fw2", [64, 64], F32)
                    fvec = sbt(fp, "fvec", [64, 6], F32)
                    arg = sbt(fp, "arg", [64, NT], F32)
                    wtmp = sbt(fp, "wtmp", [64, NT], F32)
                    h1 = sbt(fp, "h1", [64, NT], F32)
                    OP("sp", lambda e: e.dma_start(out=zT[:], in_=zT_d), writes=["zT"], dma=True)
                    OP("sp", lambda e: e.dma_start(out=fw0[:], in_=fw0_d), writes=["fw"], dma=True)
                    OP("sp", lambda e: e.dma_start(out=fw1[:], in_=fw1_d), writes=["fw"], dma=True)
                    OP("sp", lambda e: e.dma_start(out=fw2[:], in_=fw2_d), writes=["fw"], dma=True)
                    OP("sp", lambda e: e.dma_start(out=fvec[:], in_=fvec_d), writes=["fw"], dma=True)
                    layers = [(fw0, 33, zT, "zT", h1, "h1"), (fw1, 64, h1, "h1", hdnT, "hdnA"), (fw2, 64, hdnT, "hdnA", h1, "h1")]
                    for li, (fw, K, src, sres, dst, dres) in enumerate(layers):
                        for (t0, t1) in TT:
                            T = t1 - t0
                            b = next_bank(0, 7)
                            OP("pe", lambda e, fw=fw, K=K, src=src, b=b, t0=t0, t1=t1, T=T: e.matmul(
                                PS[0:64, b, 0:T], fw[0:K, :], src[0:K, t0:t1], start=True, stop=True),
                               reads=["fw", sres], writes=[BK(b)])
                            OP("dve", lambda e, b=b, t0=t0, t1=t1, T=T, li=li: e.tensor_scalar(
                                arg[:, t0:t1], PS[0:64, b, 0:T], fvec[:, li:li + 1], fvec[:, 3 + li:4 + li], ALU.add, ALU.mult),
                               reads=["fw"], writes=[BK(b), "arg"])
                        for rep in range(2):
                            OP("dve", lambda e: e.tensor_scalar(wtmp[:], arg[:], math.pi, -2 * math.pi, ALU.is_gt, ALU.mult),
                               reads=["arg"], writes=["wtmp"])
                            OP("dve", lambda e: e.tensor_tensor(arg[:], arg[:], wtmp[:], ALU.add), reads=["wtmp"], writes=["arg"])
                            OP("dve", lambda e: e.tensor_scalar(wtmp[:], arg[:], -math.pi, 2 * math.pi, ALU.is_lt, ALU.mult),
                               reads=["arg"], writes=["wtmp"])
                            OP("dve", lambda e: e.tensor_tensor(arg[:], arg[:], wtmp[:], ALU.add), reads=["wtmp"], writes=["arg"])
                        OP("act", lambda e, dst=dst: e.activation(dst[:], arg[:], AF.Sin, scale=1.0 - 2e-6),
                           reads=["arg"], writes=[dres])
                    OP("dve", lambda e: e.tensor_copy(hdnT[:], h1[:]), reads=["h1"], writes=["hdnT", "hdnA"])
                    S.barrier()
                with ExitStack() as cp:
                    prow = sbt(cp, "prow", [128, NT], BF16)
                    x0row = sbt(cp, "x0row", [128, NT], BF16)
                    zrow = sbt(cp, "zrow", [128, NT], BF16)
                    comb = sbt(cp, "comb", [128, 18, 384], BF16)
                    fwq = sbt(cp, "fwq", [64, 2, 128], F32)
                    hb0 = sbt(cp, "hb0", [64, 2, 128], F32)
                    kb = sbt(cp, "kb", [128, 2, 128], F32)
                    dl = sbt(cp, "dl", [128, 128], F32)
                    fbq = sbt(cp, "fbq", [128, 128], F32)
                    negt = sbt(cp, "negt", [128, 18], F32)
                    OP("sp", lambda e: e.dma_start(out=negt[:], in_=negt_d), writes=["negt"], dma=True)
                    wv = hy_w_in.rearrange("(kc p) m -> p kc m", p=128)
                    PT = PS[:, 7, :].bitcast(BF16)
                    fcount = [0]
                    icount = [0]
                    def hy_chunk(j):
                        OP("sp", lambda e, j=j: e.dma_start(out=fwq[:, 0, :], in_=fwout_d[:, j * 128:(j + 1) * 128]), writes=["fwq"], dma=True)
                        OP("sp", lambda e, j=j: e.dma_start(out=fwq[:, 1, :], in_=fwout_d[:, D + j * 128:D + (j + 1) * 128]), writes=["fwq"], dma=True)
                        OP("sp", lambda e, j=j: e.dma_start(out=dl[:], in_=delta_d[:, j * 128:(j + 1) * 128]), writes=["dl"], dma=True)
                        OP("sp", lambda e, j=j: e.dma_start(out=fbq[:], in_=fbias_d[:, j * 128:(j + 1) * 128]), writes=["fbq"], dma=True)
                        S.barrier()
                        with ExitStack() as sa_:
                            upads = [sbt(sa_, "upad%d" % i, [128, PADW], F32) for i in range(3)]
                            acc = sbt(sa_, "acc", [128, PADW], F32)
                            x1row = sbt(sa_, "x1row", [128, NT], BF16)
                            win = sbt(sa_, "win", [128, 3, 8, 128], BF16)
                            for i in range(3):
                                OP("pool", lambda e, i=i: e.memset(upads[i][:], 0.0), writes=["upad%d" % i])
                            for wi, oc in enumerate((8 + j, 16 + j, j)):
                                OP("pool", lambda e, wi=wi, oc=oc: e.dma_start(out=win[:, wi, :, :], in_=wv[:, :, oc * 128:(oc + 1) * 128]),
                                   writes=["win%d" % wi], dma=True)
                            def cwf(oc):
                                return lambda k: vecs[:, V_HCW + k * 24 + oc:V_HCW + k * 24 + oc + 1]
                            oc = 8 + j
                            linear_rows(lambda kc: win[:, 0, kc, :], "win0", hT, "hT", udst_of(upads[0]), "upad0", vecs[:, V_HBIN + oc:V_HBIN + oc + 1])
                            oc = 16 + j
                            linear_rows(lambda kc: win[:, 1, kc, :], "win1", hT, "hT", udst_of(upads[1]), "upad1", vecs[:, V_HBIN + oc:V_HBIN + oc + 1])
                            oc = j
                            linear_rows(lambda kc: win[:, 2, kc, :], "win2", hT, "hT", udst_of(upads[2]), "upad2", vecs[:, V_HBIN + oc:V_HBIN + oc + 1])
                            oc = 8 + j
                            conv_rows(upads[0], acc, cwf(oc), vecs[:, V_HCB + oc:V_HCB + oc + 1], 3,
                                      [(x1row[:, t:t + ln], "x1row", p, ln) for (t, p, ln) in SEGS], "dve", ures="upad0")
                            oc = 16 + j
                            conv_rows(upads[1], acc, cwf(oc), vecs[:, V_HCB + oc:V_HCB + oc + 1], 3,
                                      [(x0row[:, t:t + ln], "x0row", p, ln) for (t, p, ln) in SEGS], "pool", ures="upad1")
                            OP("dve", lambda e: e.tensor_tensor(prow[:], x1row[:], x0row[:], ALU.mult),
                               reads=["x1row", "x0row"], writes=["prow"])
                            for g0 in range(0, 18, 4):
                                gn = min(4, 18 - g0)

                                def tr(e, g0=g0, gn=gn):
                                    r = None
                                    for i in range(gn):
                                        r = e.transpose(PT[:, i * 128:(i + 1) * 128], prow[:, (g0 + i) * 128:(g0 + i + 1) * 128], ident[:])
                                    return r
                                OP("pe", tr, reads=["prow", "ident"], writes=[BK(7)])
                                for i in range(gn):
                                    OP("act", lambda e, g0=g0, i=i: e.copy(comb[:, g0 + i, 0:128], PT[:, i * 128:(i + 1) * 128]),
                                       writes=[BK(7), "comb"])
                            oc = j
                            conv_rows(upads[2], acc, cwf(oc), vecs[:, V_HCB + oc:V_HCB + oc + 1], 3,
                                      [(x0row[:, t:t + ln], "x0row", p, ln) for (t, p, ln) in SEGS], "dve", ures="upad2")
                            S.barrier()
                        with ExitStack() as sb_:
                            Y = sbt(sb_, "Y", [128, 18, 2, 128], BF16)
                            Fr = sbt(sb_, "Fr", [128, 3, 16, 128], BF16)
                            Ir = sbt(sb_, "Ir", [128, 6, 2, 512], BF16)
                            dec = sbt(sb_, "dec", [128, 2, 128], F32)
                            ev = sbt(sb_, "ev", [128, 10, 128], F32)
                            for lc in range(18):
                                b = next_bank(0, 7)
                                OP("pe", lambda e, lc=lc, b=b: e.matmul(PS[:, b, 0:256], hdnT[:, lc * 128:(lc + 1) * 128],
                                                                       fwq[:].rearrange("p a c -> p (a c)"), start=True, stop=True),
                                   reads=["hdnT", "fwq"], writes=[BK(b)])
                                OP("act", lambda e, lc=lc: e.activation(dec[:, lc % 2, :], dl[:], AF.Exp, scale=negt[:, lc:lc + 1]),
                                   reads=["dl", "negt"], writes=["dec%d" % (lc % 2)])
                                for a in range(2):
                                    OP("dve", lambda e, lc=lc, b=b, a=a: e.tensor_tensor(
                                        comb[:, lc, 128 + a * 128:256 + a * 128], PS[:, b, a * 128:(a + 1) * 128], dec[:, lc % 2, :], ALU.mult),
                                       reads=["dec%d" % (lc % 2)], writes=[BK(b), "comb"])
                            for si, col in enumerate((0, CTX)):
                                OP("dve", lambda e, si=si, col=col: e.tensor_scalar(
                                    hb0[:, si, :], fwq[:, 1, :], hdnT[:, col:col + 1], None, ALU.mult),
                                   reads=["hdnT", "fwq"], writes=["hb0"])
                                b = next_bank(0, 7)
                                OP("pe", lambda e, si=si, b=b: e.matmul(PS[:, b, 0:128], ones_f32[:, :], hb0[:, si, :], start=True, stop=True),
                                   reads=["hb0", "ones_f32"], writes=[BK(b)])
                                OP("dve", lambda e, si=si, b=b: e.tensor_tensor(kb[:, si, :], fbq[:], PS[:, b, 0:128], ALU.subtract),
                                   reads=["fbq"], writes=[BK(b), "kb"])
                            for (si, lc0, nlc) in ((0, 0, 2), (1, 2, 16)):
                                for fc in range(nlc):
                                    banks = []
                                    for ab in range(2):
                                        fs = fcount[0] % 3
                                        fcount[0] += 1
                                        if si == 0:
                                            OP("sp", lambda e, fc=fc, ab=ab, fs=fs: e.dma_start(out=Fr[:, fs, 0:2, :], in_=dftFc[fc, ab]),
                                               writes=["Fr%d" % fs], dma=True)
                                        else:
                                            OP("sp", lambda e, fc=fc, ab=ab, fs=fs: e.dma_start(out=Fr[:, fs, :, :], in_=dftF[fc, ab]),
                                               writes=["Fr%d" % fs], dma=True)
                                        b = next_bank(0, 7)
                                        banks.append(b)

                                        def mm(e, fs=fs, b=b, lc0=lc0, nlc=nlc):
                                            r = None
                                            for nc_ in range(nlc):
                                                r = e.matmul(PS[:, b, 0:384], Fr[:, fs, nc_, :], comb[:, lc0 + nc_, :],
                                                             start=(nc_ == 0), stop=(nc_ == nlc - 1))
                                            return r
                                        OP("pe", mm, reads=["Fr%d" % fs, "comb"], writes=[BK(b)])
                                    bA, bB = banks
                                    yi = lc0 + fc
                                    E = lambda i: ev[:, i, :]
                                    OP("act", lambda e, bA=bA: e.copy(E(0), PS[:, bA, 0:128]), writes=[BK(bA), "ev0"])
                                    OP("act", lambda e, bB=bB: e.copy(E(1), PS[:, bB, 0:128]), writes=[BK(bB), "ev1"])
                                    OP("act", lambda e, bB=bB: e.copy(E(2), PS[:, bB, 256:384]), writes=[BK(bB), "ev2"])
                                    OP("dve", lambda e, bA=bA, si=si: e.tensor_tensor(E(3), PS[:, bA, 128:256], kb[:, si, :], ALU.add),
                                       reads=["kb"], writes=[BK(bA), "ev3"])
                                    OP("dve", lambda e, bA=bA: e.tensor_tensor(E(3), E(3), PS[:, bA, 256:384], ALU.add),
                                       writes=[BK(bA), "ev3"])
                                    OP("dve", lambda e, bB=bB: e.tensor_tensor(E(4), PS[:, bB, 128:256], E(2), ALU.subtract),
                                       reads=["ev2"], writes=[BK(bB), "ev4"])
                                    OP("dve", lambda e: e.tensor_tensor(E(5), E(0), E(3), ALU.mult), reads=["ev0", "ev3"], writes=["ev5"])
                                    OP("dve", lambda e: e.tensor_tensor(E(6), E(1), E(4), ALU.mult), reads=["ev1", "ev4"], writes=["ev6"])
                                    OP("dve", lambda e, yi=yi: e.tensor_tensor(Y[:, yi, 0, :], E(5), E(6), ALU.subtract),
                                       reads=["ev5", "ev6"], writes=["Y"])
                                    OP("pool", lambda e: e.tensor_tensor(E(7), E(0), E(4), ALU.mult), reads=["ev0", "ev4"], writes=["ev7"])
                                    OP("pool", lambda e: e.tensor_tensor(E(8), E(1), E(3), ALU.mult), reads=["ev1", "ev3"], writes=["ev8"])
                                    OP("pool", lambda e, yi=yi: e.tensor_tensor(Y[:, yi, 1, :], E(7), E(8), ALU.add),
                                       reads=["ev7", "ev8"], writes=["Y"])
                            b = next_bank(0, 7)
                            for fc in range(2):
                                is_ = icount[0] % 6
                                icount[0] += 1
                                OP("sp", lambda e, fc=fc, is_=is_: e.dma_start(out=Ir[:, is_, :, 0:256], in_=dftIc[fc]),
                                   writes=["Ir%d" % is_], dma=True)

                                def mm(e, fc=fc, is_=is_, b=b):
                                    r = None
                                    for ri in range(2):
                                        r = e.matmul(PS[:, b, 0:256], Y[:, fc, ri, :], Ir[:, is_, ri, 0:256],
                                                     start=(fc == 0 and ri == 0), stop=(fc == 1 and ri == 1))
                                    return r
                                OP("pe", mm, reads=["Y", "Ir%d" % is_], writes=[BK(b)])
                            OP("dve", lambda e, b=b: e.tensor_tensor(zrow[:, 0:CTX], x0row[:, 0:CTX], PS[:, b, 0:256], ALU.mult),
                               reads=["x0row"], writes=[BK(b), "zrow"])
                            for nt in range(4):
                                b = next_bank(0, 7)
                                for fc in range(16):
                                    is_ = icount[0] % 6
                                    icount[0] += 1
                                    OP("sp", lambda e, fc=fc, is_=is_, nt=nt: e.dma_start(out=Ir[:, is_, :, :], in_=dftI[nt, fc]),
                                       writes=["Ir%d" % is_], dma=True)

                                    def mm(e, fc=fc, is_=is_, b=b):
                                        r = None
                                        for ri in range(2):
                                            r = e.matmul(PS[:, b, :], Y[:, 2 + fc, ri, :], Ir[:, is_, ri, :],
                                                         start=(fc == 0 and ri == 0), stop=(fc == 15 and ri == 1))
                                        return r
                                    OP("pe", mm, reads=["Y", "Ir%d" % is_], writes=[BK(b)])
                                t0 = CTX + nt * 512
                                OP("dve", lambda e, b=b, t0=t0: e.tensor_tensor(zrow[:, t0:t0 + 512], x0row[:, t0:t0 + 512], PS[:, b, :], ALU.mult),
                                   reads=["x0row"], writes=[BK(b), "zrow"])
                            OP("sp", lambda e, j=j: e.dma_start(out=zs[:, j, :], in_=zrow[:]), reads=["zrow"], writes=["zs"], dma=True)
                            S.barrier()
                    for j in range(8):
                        hy_chunk(j)
                OP("sp", lambda e: e.dma_start(out=hT[:], in_=zs), reads=["zs"], writes=["hT"], dma=True)
                mixer_out(hy_w_out, hT, "hT", V_HBOUT, list(range(9)), l, ada_next=1)

        def rglru(l):
            S.barrier()
            with ExitStack() as ph:
                hT = sbt(ph, "hT", [128, 8, NT], BF16)
                prenorm_all(hT, 3, l)
                xcb = sbt(ph, "xcb", [128, 2, NT], BF16)
                OP("act", lambda e: e.activation(softp[:], vecs[:, V_RLAM:V_RLAM + 16], AF.Exp, scale=-1.0), reads=["vecs"], writes=["softp"])
                OP("act", lambda e: e.activation(softp[:], softp[:], AF.Ln, bias=1.0), writes=["softp"])
                OP("dve", lambda e: e.tensor_scalar(softp[:], softp[:], -8.0, None, ALU.mult), writes=["softp"])
                wv = rg_w_in.rearrange("(kc p) m -> p kc m", p=128)
                def rg_head(hd):
                    S.barrier()
                    with ExitStack() as sa_:
                        upad = sbt(sa_, "upad", [128, PADW], F32)
                        acc = sbt(sa_, "acc", [128, PADW], F32)
                        win = sbt(sa_, "win", [128, 2, 8, 128], BF16)
                        OP("pool", lambda e: e.memset(upad[:], 0.0), writes=["upad"])
                        udst = udst_of(upad)
                        for jj in range(2):
                            j = 2 * hd + jj
                            oc = 8 + j
                            OP("pool", lambda e, oc=oc, jj=jj: e.dma_start(out=win[:, jj, :, :], in_=wv[:, :, oc * 128:(oc + 1) * 128]),
                               writes=["win%d" % jj], dma=True)
                            linear_rows(lambda kc, jj=jj: win[:, jj, kc, :], "win%d" % jj, hT, "hT", udst, "upad", vecs[:, V_RBIN + oc:V_RBIN + oc + 1])
                            conv_rows(upad, acc, lambda k, j=j: vecs[:, V_RCW + k * 8 + j:V_RCW + k * 8 + j + 1],
                                      vecs[:, V_RCB + j:V_RCB + j + 1], 4,
                                      [(xcb[:, jj, t:t + ln], "xcb", p, ln) for (t, p, ln) in SEGS], "pool")
                        S.barrier()
                    with ExitStack() as sb_:
                        wing = sbt(sb_, "wing", [128, 8, 128], BF16)
                        wg = sbt(sb_, "wg", [128, 2, 2, 2, 256], BF16)
                        rr = sbt(sb_, "rr", [128, NT], F32)
                        ir = sbt(sb_, "ir", [128, NT], F32)
                        ar = sbt(sb_, "ar", [128, NT], F32)
                        hsA = sbt(sb_, "hsA", [128, NT], F32)
                        gg = sbt(sb_, "gg", [128, LAT], F32)
                        zrow = sbt(sb_, "zrow", [128, LAT], BF16)
                        g2 = ar
                        for d in range(2):
                            for gi, wsrc in enumerate((rg_wa, rg_wi)):
                                OP("pool", lambda e, d=d, gi=gi, wsrc=wsrc, hd=hd: e.dma_start(
                                    out=wg[:, d, gi, :, :], in_=wsrc[d, hd].rearrange("(kc p) o -> p kc o", p=128)),
                                   writes=["wg"], dma=True)
                        for jj in range(2):
                            j = 2 * hd + jj
                            OP("pool", lambda e, j=j: e.dma_start(out=wing[:], in_=wv[:, :, j * 128:(j + 1) * 128]),
                               writes=["wing"], dma=True)
                            linear_rows(lambda kc: wing[:, kc, :], "wing", hT, "hT",
                                        lambda t0, t1: gg[:, t0 - CTX:t1 - CTX], "gg", vecs[:, V_RBIN + j:V_RBIN + j + 1], tiles=TT[1:])
                            G2 = g2[:, 0:LAT]
                            OP("act", lambda e: e.activation(G2, gg[:], AF.Square), reads=["gg"], writes=["ar"])
                            OP("pool", lambda e: e.tensor_scalar(G2, G2, 0.044715, 1.0, ALU.mult, ALU.add), writes=["ar"])
                            OP("pool", lambda e: e.tensor_tensor(G2, G2, gg[:], ALU.mult), reads=["gg"], writes=["ar"])
                            OP("act", lambda e: e.activation(G2, G2, AF.Sigmoid, scale=1.5957691216057308), writes=["ar"])
                            OP("pool", lambda e: e.tensor_tensor(gg[:], gg[:], G2, ALU.mult), reads=["ar"], writes=["gg"])
                            for d in range(2):
                                for gi, (dstrow, dres, boff) in enumerate(((rr, "rr", V_RBA), (ir, "ir", V_RBI))):
                                    for (t0, t1) in TT:
                                        T = t1 - t0
                                        b = next_bank(0, 7)

                                        def mm(e, d=d, gi=gi, jj=jj, b=b, t0=t0, t1=t1, T=T):
                                            r = None
                                            for kc in range(2):
                                                r = e.matmul(PS[:, b, 0:T], wg[:, d, gi, kc, jj * 128:(jj + 1) * 128],
                                                             xcb[:, kc, t0:t1], start=(kc == 0), stop=(kc == 1))
                                            return r
                                        OP("pe", mm, reads=["wg", "xcb"], writes=[BK(b)])
                                        OP("act", lambda e, dstrow=dstrow, d=d, j=j, boff=boff, b=b, t0=t0, t1=t1, T=T: e.activation(
                                            dstrow[:, t0:t1], PS[:, b, 0:T], AF.Sigmoid,
                                            bias=vecs[:, boff + d * 8 + j:boff + d * 8 + j + 1], scale=1.0),
                                           reads=["vecs"], writes=[BK(b), dres])
                                OP("act", lambda e, d=d, j=j: e.activation(ar[:], rr[:], AF.Exp, scale=softp[:, d * 8 + j:d * 8 + j + 1]),
                                   reads=["rr", "softp"], writes=["ar"])
                                OP("dve", lambda e, jj=jj: e.tensor_tensor(ir[:], ir[:], xcb[:, jj, :], ALU.mult), reads=["xcb"], writes=["ir"])
                                OP("act", lambda e: e.activation(rr[:], ar[:], AF.Square), reads=["ar"], writes=["rr"])
                                OP("act", lambda e: e.activation(rr[:], rr[:], AF.Sqrt, bias=1.0, scale=-1.0), writes=["rr"])
                                OP("dve", lambda e: e.tensor_tensor(ir[:], ir[:], rr[:], ALU.mult), reads=["rr"], writes=["ir"])
                                if d == 0:
                                    OP("dve", lambda e: e.tensor_tensor_scan(hsA[:, 0:CTX], ar[:, 0:CTX], ir[:, 0:CTX], 0.0, ALU.mult, ALU.add),
                                       reads=["ar", "ir"], writes=["hsAc"])
                                    OP("dve", lambda e: e.tensor_tensor_scan(hsA[:, CTX:NT], ar[:, CTX:NT], ir[:, CTX:NT],
                                                                            hsA[:, CTX - 1:CTX], ALU.mult, ALU.add),
                                       reads=["ar", "ir", "hsAc"], writes=["hsA"])
                                else:
                                    OP("dve", lambda e: e.tensor_tensor_scan(rr[:, CTX - 1::-1], ar[:, CTX - 1::-1], ir[:, CTX - 1::-1], 0.0, ALU.mult, ALU.add),
                                       reads=["ar", "ir"], writes=["rr"])
                                    OP("dve", lambda e: e.tensor_tensor_scan(rr[:, NT - 1:CTX - 1:-1], ar[:, NT - 1:CTX - 1:-1], ir[:, NT - 1:CTX - 1:-1],
                                                                            rr[:, 0:1], ALU.mult, ALU.add),
                                       reads=["ar", "ir"], writes=["rr"])
                            OP("dve", lambda e: e.tensor_tensor(hsA[:, CTX:NT], hsA[:, CTX:NT], rr[:, CTX:NT], ALU.add),
                               reads=["rr"], writes=["hsA"])
                            OP("dve", lambda e: e.tensor_tensor(zrow[:], hsA[:, CTX:NT], gg[:], ALU.mult),
                               reads=["hsA", "gg"], writes=["zrow"])
                            OP("sp", lambda e, j=j: e.dma_start(out=zs[:, j, CTX:NT], in_=zrow[:]), reads=["zrow"], writes=["zs"], dma=True)
                        S.barrier()
                for hd in range(4):
                    rg_head(hd)
                OP("sp", lambda e: e.dma_start(out=hT[:], in_=zs), reads=["zs"], writes=["hT"], dma=True)
                mixer_out(rg_w_out, hT, "hT", V_RBOUT, list(range(1, 9)), l)

        nst = 0

        def stage():
            nonlocal nst
            nst += 1
            return nst <= stages
        if stage():
            ffn(0, 0, list(range(9)), pre=lambda wl: ada_layer(0, skip=wl))
        else:
            ada_layer(0)
        if stage():
            hyena(0)
        if stage():
            ffn(0, 1, list(range(9)))
        if stage():
            ffn(1, 0, list(range(9)))
        if stage():
            rglru(1)
        if stage():
            ffn(1, 1, list(range(1, 9)))
        for c in range(8):
            o = OP("sp", lambda e, c=c: e.dma_start(out=out_d[:, c, :], in_=xT[:, c, CTX:NT]),
                   reads=xres_range(c, CTX, NT), dma=True)
            out_ops.append(o)
        S.finalize(final_wait_ops=out_ops)
    return nc


def _fm(v):
    v = np.asarray(v, np.float32)
    n = v.shape[-1] // 128
    v = v.reshape(v.shape[:-1] + (n, 128))
    return np.moveaxis(v, -1, 0)


def _pos_embed():
    rows = LAT // 64
    row = np.repeat(np.arange(rows, dtype=np.float32), 64)
    col = np.tile(np.arange(64, dtype=np.float32), rows)
    quarter = D // 4
    omega = (10000.0 ** (-np.arange(quarter, dtype=np.float32) / quarter)).astype(np.float32)

    def emb(p):
        ang = p[:, None] * omega[None]
        return np.concatenate([np.sin(ang), np.cos(ang)], axis=-1)
    return np.concatenate([emb(row), emb(col)], axis=-1).astype(np.float32)


def _zfeat(L):
    t = np.linspace(0.0, 1.0, L, dtype=np.float32)[:, None]
    w = (2.0 * math.pi * np.arange(L, dtype=np.float32)[:, None] / L).astype(np.float32)
    f = np.linspace(1e-4, 15, 16, dtype=np.float32)[None]
    phase = f * w
    return np.concatenate([t, np.cos(phase), -np.sin(phase)], axis=-1).astype(np.float32)


def _dft(L):
    N = 2 * L
    n = np.arange(L, dtype=np.float64)[:, None]
    f = np.arange(L, dtype=np.float64)[None, :]
    th = 2 * np.pi * (f + 0.5) * n / N
    return np.cos(th), -np.sin(th)


_CONST = {}


def _constants():
    if _CONST:
        return _CONST
    bf = ml_dtypes.bfloat16
    pe = _pos_embed()
    _CONST["pos"] = np.ascontiguousarray(pe.T.reshape(8, 128, LAT).transpose(1, 0, 2))
    zT = np.concatenate([_zfeat(CTX), _zfeat(LAT)], axis=0).T
    _CONST["zT"] = np.ascontiguousarray(zT, np.float32)
    min_decay = math.log(1e-2) / 1.5
    max_decay = math.log(1e-2) / 0.3
    deltas = np.abs(np.linspace(min_decay, max_decay, D, dtype=np.float32))
    _CONST["delta_b"] = np.ascontiguousarray(np.broadcast_to(deltas[None, :], (128, D)), np.float32)
    negt = np.zeros((128, 18), np.float32)
    tc = np.linspace(0.0, 1.0, CTX, dtype=np.float32)
    tl = np.linspace(0.0, 1.0, LAT, dtype=np.float32)
    negt[:, 0:2] = -tc.reshape(2, 128).T
    negt[:, 2:18] = -tl.reshape(16, 128).T
    _CONST["negt"] = negt
    A, B = _dft(LAT)
    M = np.stack([A, B])
    F = M.reshape(2, 16, 128, 16, 128).transpose(3, 0, 2, 1, 4)
    _CONST["dftF"] = np.ascontiguousarray(F).astype(bf)
    I = (M * (2.0 / (2 * LAT))).reshape(2, 4, 512, 16, 128).transpose(1, 3, 4, 0, 2)
    _CONST["dftI"] = np.ascontiguousarray(I).astype(bf)
    A, B = _dft(CTX)
    M = np.stack([A, B])
    F = M.reshape(2, 2, 128, 2, 128).transpose(3, 0, 2, 1, 4)
    _CONST["dftFc"] = np.ascontiguousarray(F).astype(bf)
    I = (M * (2.0 / (2 * CTX))).reshape(2, 256, 2, 128).transpose(2, 3, 0, 1)
    _CONST["dftIc"] = np.ascontiguousarray(I).astype(bf)
    return _CONST


_NC_CACHE = {}


def kernel(**inp):
    f32 = lambda a: np.ascontiguousarray(np.asarray(a, np.float32))
    cst = _constants()
    B = inp["x"].shape[0]
    vec = np.zeros((128, NV), np.float32)
    vec[:, V_ADAB:V_ADAB + 144] = _fm(inp["ada_b"]).reshape(128, 144)
    vec[:, V_NG:V_NG + 96] = _fm(inp["norm_g"]).reshape(128, 96)
    vec[:, V_HBIN:V_HBIN + 24] = _fm(inp["hy_b_in"][0])
    vec[:, V_HCW:V_HCW + 72] = _fm(inp["hy_conv_w"][0]).reshape(128, 72)
    vec[:, V_HCB:V_HCB + 24] = _fm(inp["hy_conv_b"][0])
    vec[:, V_HBOUT:V_HBOUT + 8] = _fm(inp["hy_b_out"][0])
    vec[:, V_RBIN:V_RBIN + 16] = _fm(inp["rg_b_in"][0])
    vec[:, V_RCW:V_RCW + 32] = _fm(inp["rg_conv_w"][0]).reshape(128, 32)
    vec[:, V_RCB:V_RCB + 8] = _fm(inp["rg_conv_b"][0])
    vec[:, V_RBA:V_RBA + 16] = _fm(inp["rg_ba"][0]).reshape(128, 16)
    vec[:, V_RBI:V_RBI + 16] = _fm(inp["rg_bi"][0]).reshape(128, 16)
    vec[:, V_RLAM:V_RLAM + 16] = _fm(inp["rg_lam"][0]).reshape(128, 16)
    vec[:, V_RBOUT:V_RBOUT + 8] = _fm(inp["rg_b_out"][0])
    fvec = np.stack([inp["hy_fb0"][0], inp["hy_fb1"][0], inp["hy_fb2"][0],
                     inp["hy_freq"][0, 0], inp["hy_freq"][0, 1], inp["hy_freq"][0, 2]], axis=1)
    shared = {
        "pos": cst["pos"], "ada_w": f32(inp["ada_w"]), "vecs": vec,
        "ffn_w1": f32(inp["ffn_w1"]), "ffn_w3": f32(inp["ffn_w3"]), "ffn_w2": f32(inp["ffn_w2"]),
        "hy_w_in": f32(inp["hy_w_in"][0]), "hy_w_out": f32(inp["hy_w_out"][0]),
        "rg_w_in": f32(inp["rg_w_in"][0]), "rg_w_out": f32(inp["rg_w_out"][0]),
        "rg_wa": f32(inp["rg_wa"][0]), "rg_wi": f32(inp["rg_wi"][0]),
        "zT": cst["zT"], "fw0": f32(inp["hy_fw0"][0]), "fw1": f32(inp["hy_fw1"][0]), "fw2": f32(inp["hy_fw2"][0]),
        "fvec": f32(fvec), "fwout": f32(inp["hy_fwout"][0]),
        "delta_b": cst["delta_b"], "negt": cst["negt"],
        "fbias_b": f32(np.broadcast_to(np.asarray(inp["hy_filt_bias"][0], np.float32)[None, :], (128, D))),
        "dftF": cst["dftF"], "dftFc": cst["dftFc"], "dftI": cst["dftI"], "dftIc": cst["dftIc"],
    }
    cctx = np.asarray(inp["c_ctx"], np.float32)
    in_maps = []
    for b in range(B):
        m = dict(shared)
        m["xin"] = f32(np.asarray(inp["x"][b], np.float32).T.reshape(8, 128, LAT).transpose(1, 0, 2))
        m["cin"] = f32(np.asarray(inp["ctx"][b], np.float32).T.reshape(8, 128, CTX).transpose(1, 0, 2))
        cv = np.stack([np.asarray(inp["c"][b], np.float32), cctx], axis=-1)
        m["cvec"] = f32(cv.reshape(8, 128, 2).transpose(1, 0, 2))
        in_maps.append(m)
    if "nc" not in _NC_CACHE:
        _NC_CACHE["nc"] = build_program()
    res = run_bass_kernel_spmd(_NC_CACHE["nc"], in_maps, core_ids=list(range(B)))
    outs = []
    for b in range(B):
        o = np.asarray(res.results[b]["out"], np.float32)
        outs.append(o.transpose(1, 0, 2).reshape(D, LAT).T)
    return np.ascontiguousarray(np.stack(outs, axis=0), np.float32)
```
